# Optimizing a Trainium2 kernel written in Bass

```python
import math
import jax, jax.numpy as jnp
from jax import lax
import numpy as np

D_MODEL = 1024
BATCH = 2
SEQ = 8192
DEPTH = 2

GRID_W = 64
CTX_LEN = 256
HEAD_DIM = 64
A_HEADS = 8
A_KV_HEADS = 2
A_GROUP = A_HEADS // A_KV_HEADS
B_HEADS = 4
B_V_DIM = 2 * HEAD_DIM
C_GROUPS = 4
C_GROUP_DIM = 128
N_BRANCH = 3
BRANCH_W = 512
A_Q_W = A_HEADS * HEAD_DIM
A_KV_W = A_KV_HEADS * HEAD_DIM
B_QK_W = B_HEADS * 2 * HEAD_DIM
B_V_W = B_HEADS * B_V_DIM
C_W = C_GROUPS * C_GROUP_DIM
GATE_W = N_BRANCH * D_MODEL
IN_W = A_Q_W + 2 * A_KV_W + 2 * B_QK_W + B_V_W + C_W + GATE_W
D_FF = ((8 * D_MODEL // 3 + 127) // 128) * 128
N_MOD = 9
Q_BLOCK = 128
ROPE_THETA = 10000.0
EPS = 1e-6

kernel_name = "hybrid_gqa_diffattn_fourier_macaron_dit"


def rmsnorm(x, g):
    xf = x.astype(jnp.float32)
    y = xf * lax.rsqrt(jnp.mean(xf * xf, axis=-1, keepdims=True) + EPS)
    return (y * g.astype(jnp.float32)).astype(x.dtype)


def modulate(x, shift, scale):
    return x * (1 + scale[:, None, :]) + shift[:, None, :]


def swiglu(x, wi, wo):
    g, u = jnp.split(x @ wi, 2, axis=-1)
    return (jax.nn.silu(g) * u) @ wo


def ffn_half(h, m, k, g_norm, wi, wo):
    hn = modulate(rmsnorm(h, g_norm), m[3 * k], m[3 * k + 1])
    return h + 0.5 * m[3 * k + 2][:, None, :] * swiglu(hn, wi, wo)


def axial_rope(n_tok):
    rows = n_tok // GRID_W
    row = jnp.repeat(jnp.arange(rows, dtype=jnp.float32), GRID_W)
    col = jnp.tile(jnp.arange(GRID_W, dtype=jnp.float32), rows)
    n_freq = HEAD_DIM // 4
    inv = jnp.power(ROPE_THETA, -jnp.arange(n_freq, dtype=jnp.float32) / n_freq)
    ang = jnp.concatenate([row[:, None] * inv, col[:, None] * inv], axis=-1)
    return jnp.cos(ang), jnp.sin(ang)


def apply_rope(x, cos, sin):
    shp = (1, x.shape[1]) + (1,) * (x.ndim - 3) + (x.shape[-1] // 2,)
    cs = cos.reshape(shp).astype(x.dtype)
    sn = sin.reshape(shp).astype(x.dtype)
    x1, x2 = jnp.split(x, 2, axis=-1)
    return jnp.concatenate([x1 * cs - x2 * sn, x1 * sn + x2 * cs], axis=-1)


def split_projection(z):
    sizes = (A_Q_W, A_KV_W, A_KV_W, B_QK_W, B_QK_W, B_V_W, C_W)
    idx = []
    o = 0
    for s in sizes:
        o += s
        idx.append(o)
    return jnp.split(z, idx, axis=-1)


def project_tokens(hn, w_in, qk_g, cos, sin):
    bn, n = hn.shape[:2]
    qa, ka, va, qb, kb, vb, uc, gl = split_projection(hn @ w_in)
    qa = rmsnorm(qa.reshape(bn, n, A_KV_HEADS, A_GROUP, HEAD_DIM), qk_g[0])
    ka = rmsnorm(ka.reshape(bn, n, A_KV_HEADS, HEAD_DIM), qk_g[1])
    va = va.reshape(bn, n, A_KV_HEADS, HEAD_DIM)
    qb = rmsnorm(qb.reshape(bn, n, B_HEADS, 2, HEAD_DIM), qk_g[2])
    kb = rmsnorm(kb.reshape(bn, n, B_HEADS, 2, HEAD_DIM), qk_g[3])
    vb = vb.reshape(bn, n, B_HEADS, B_V_DIM)
    if cos is not None:
        qa = apply_rope(qa, cos, sin)
        ka = apply_rope(ka, cos, sin)
        qb = apply_rope(qb, cos, sin)
        kb = apply_rope(kb, cos, sin)
    uc = uc.reshape(bn, n, C_GROUPS, C_GROUP_DIM)
    return qa, ka, va, qb, kb, vb, uc, gl


def gqa_attend(q, k, v):
    s = jnp.einsum('bqhgd,bkhd->bhgqk', q.astype(jnp.float32) * HEAD_DIM ** -0.5, k.astype(jnp.float32))
    p = jax.nn.softmax(s, axis=-1)
    return jnp.einsum('bhgqk,bkhd->bqhgd', p.astype(v.dtype), v)


def diff_attend(q, k, v, lam):
    s = jnp.einsum('bqhmd,bkhmd->bhmqk', q.astype(jnp.float32) * HEAD_DIM ** -0.5, k.astype(jnp.float32))
    p = jax.nn.softmax(s, axis=-1)
    p = p[:, :, 0] - lam * p[:, :, 1]
    return jnp.einsum('bhqk,bkhe->bqhe', p.astype(v.dtype), v)


def sweep_query_blocks(fn, q):
    bn, s = q.shape[:2]
    nb = s // Q_BLOCK
    blocks = jnp.moveaxis(q.reshape((bn, nb, Q_BLOCK) + q.shape[2:]), 1, 0)
    out = jnp.moveaxis(lax.map(fn, blocks), 0, 1)
    return out.reshape((bn, s) + out.shape[3:])


def diff_lambda(lp, lam_init):
    lpf = lp.astype(jnp.float32)
    return jnp.exp(jnp.sum(lpf[0] * lpf[1])) - jnp.exp(jnp.sum(lpf[2] * lpf[3])) + lam_init


def fourier_mix(u):
    f = jnp.fft.fft2(u.astype(jnp.float32), axes=(1, 3), norm='ortho')
    return jnp.real(f).astype(u.dtype)


def merge_branches(ya, yb, yc, gl, w_branch, w_out):
    bn, n = ya.shape[:2]
    br = jnp.stack([ya.reshape(bn, n, BRANCH_W), yb.reshape(bn, n, BRANCH_W), yc.reshape(bn, n, BRANCH_W)], axis=2)
    proj = jnp.einsum('bnie,ied->bnid', br, w_branch)
    gates = jax.nn.sigmoid(gl.reshape(bn, n, N_BRANCH, D_MODEL))
    return jnp.sum(gates * proj, axis=2) @ w_out


def setup_inputs(seed: int = 0) -> dict:
    key = jax.random.key(seed)
    ks = jax.random.split(key, 16)
    f32 = jnp.float32
    d = D_MODEL
    x = jax.random.normal(ks[0], (BATCH, SEQ, d), f32)
    c = jax.random.normal(ks[1], (BATCH, d), f32)
    ctx = jax.random.normal(ks[2], (BATCH, CTX_LEN, d), f32)
    c_ctx = jax.random.normal(ks[3], (d,), f32)
    w_ada = jax.random.normal(ks[4], (DEPTH, d, N_MOD * d), f32) * (0.5 * d ** -0.5)
    b_ada = jax.random.normal(ks[5], (DEPTH, N_MOD * d), f32) * 0.02
    norm_g = 1.0 + 0.05 * jax.random.normal(ks[6], (DEPTH, 3, d), f32)
    ffn_wi = jax.random.normal(ks[7], (DEPTH, 2, d, 2 * D_FF), f32) * d ** -0.5
    ffn_wo = jax.random.normal(ks[8], (DEPTH, 2, D_FF, d), f32) * D_FF ** -0.5
    w_in = jax.random.normal(ks[9], (DEPTH, d, IN_W), f32) * d ** -0.5
    qk_g = 1.0 + 0.05 * jax.random.normal(ks[10], (DEPTH, 4, HEAD_DIM), f32)
    diff_lam = 0.1 * jax.random.normal(ks[11], (DEPTH, 4, HEAD_DIM), f32)
    diff_subln_g = 1.0 + 0.05 * jax.random.normal(ks[12], (DEPTH, B_V_DIM), f32)
    w_branch = jax.random.normal(ks[13], (DEPTH, N_BRANCH, BRANCH_W, d), f32) * BRANCH_W ** -0.5
    w_out = jax.random.normal(ks[14], (DEPTH, d, d), f32) * d ** -0.5
    return {"x": x, "c": c, "ctx": ctx, "c_ctx": c_ctx, "w_ada": w_ada, "b_ada": b_ada,
            "norm_g": norm_g, "ffn_wi": ffn_wi, "ffn_wo": ffn_wo, "w_in": w_in, "qk_g": qk_g,
            "diff_lam": diff_lam, "diff_subln_g": diff_subln_g, "w_branch": w_branch, "w_out": w_out}


def reference(x, c, ctx, c_ctx, w_ada, b_ada, norm_g, ffn_wi, ffn_wo, w_in, qk_g,
              diff_lam, diff_subln_g, w_branch, w_out):
    cos, sin = axial_rope(x.shape[1])
    h, hc = x, ctx
    for l in range(DEPTH):
        last = l == DEPTH - 1
        ml = jnp.split(jax.nn.silu(c) @ w_ada[l] + b_ada[l], N_MOD, axis=-1)
        mc = jnp.split(jax.nn.silu(c_ctx)[None] @ w_ada[l] + b_ada[l], N_MOD, axis=-1)

        h = ffn_half(h, ml, 0, norm_g[l, 0], ffn_wi[l, 0], ffn_wo[l, 0])
        hc = ffn_half(hc, mc, 0, norm_g[l, 0], ffn_wi[l, 0], ffn_wo[l, 0])

        hn = modulate(rmsnorm(h, norm_g[l, 1]), ml[3], ml[4])
        hcn = modulate(rmsnorm(hc, norm_g[l, 1]), mc[3], mc[4])
        qa, ka, va, qb, kb, vb, uc, gl = project_tokens(hn, w_in[l], qk_g[l], cos, sin)
        qa_c, ka_c, va_c, qb_c, kb_c, vb_c, uc_c, gl_c = project_tokens(hcn, w_in[l], qk_g[l], None, None)
        lam_init = 0.8 - 0.6 * math.exp(-0.3 * l)
        lam = diff_lambda(diff_lam[l], lam_init)

        ka_all = jnp.concatenate([ka_c, ka], axis=1)
        va_all = jnp.concatenate([va_c, va], axis=1)
        kb_all = jnp.concatenate([kb_c, kb], axis=1)
        vb_all = jnp.concatenate([vb_c, vb], axis=1)
        ya = sweep_query_blocks(lambda qblk: gqa_attend(qblk, ka_all, va_all), qa)
        yb = sweep_query_blocks(lambda qblk: diff_attend(qblk, kb_all, vb_all, lam), qb)
        yb = rmsnorm(yb, diff_subln_g[l]) * (1.0 - lam_init)
        yc = fourier_mix(uc)
        h = h + ml[5][:, None, :] * merge_branches(ya, yb, yc, gl, w_branch[l], w_out[l])

        if not last:
            ya_c = gqa_attend(qa_c, ka_c, va_c)
            yb_c = rmsnorm(diff_attend(qb_c, kb_c, vb_c, lam), diff_subln_g[l]) * (1.0 - lam_init)
            yc_c = fourier_mix(uc_c)
            hc = hc + mc[5][:, None, :] * merge_branches(ya_c, yb_c, yc_c, gl_c, w_branch[l], w_out[l])

        h = ffn_half(h, ml, 2, norm_g[l, 2], ffn_wi[l, 1], ffn_wo[l, 1])
        if not last:
            hc = ffn_half(hc, mc, 2, norm_g[l, 2], ffn_wi[l, 1], ffn_wo[l, 1])
    return h
```

```python
import contextlib
import math
import numpy as np
import ml_dtypes
import concourse.bass as bass
import concourse.mybir as mybir
from concourse.bass_utils import run_bass_kernel_spmd

F32 = mybir.dt.float32
BF16 = mybir.dt.bfloat16
AF = mybir.ActivationFunctionType
ALU = mybir.AluOpType
NPBF = ml_dtypes.bfloat16

D = 1024
SEQ = 8192
CTX = 256
DFF = 2816
INW = 5888
NCORE = 8
TL = 2048
TC = 64
T = TL + TC
NKEY = SEQ + CTX
EPS = 1e-6
BLKS = [(0, 512), (512, 512), (1024, 512), (1536, 512), (2048, 64)]
TGS = [[0, 1], [2, 3, 4]]

ENGS = ("pe", "act", "dve", "pool", "sp")
SEM_CAP = 30000
DMA_POOL = 20
DEBUG = False


class Sched:
    def __init__(self, nc):
        self.nc = nc
        self.ops = []
        self.lastw = {}
        self.readers = {}
        self.last_of = {e: None for e in ENGS}
        self.open_dma = set()

    def op(self, eng, fn, reads=(), writes=(), dma=False, extra=()):
        i = len(self.ops)
        deps = set(extra)
        for t in reads:
            w = self.lastw.get(t)
            if w is not None:
                deps.add(w)
        for t in writes:
            w = self.lastw.get(t)
            if w is not None:
                deps.add(w)
            rd = self.readers.get(t)
            if rd:
                deps.update(rd.values())
        self.ops.append(dict(eng=eng, fn=fn, dma=dma, deps=deps, inc=False))
        for t in reads:
            key = ("dma", i) if dma else eng
            self.readers.setdefault(t, {})[key] = i
        for t in writes:
            self.lastw[t] = i
            self.readers[t] = {}
        if fn is not None:
            self.last_of[eng] = i
        if dma:
            self.open_dma.add(i)
        return i

    def barrier(self):
        deps = set(v for v in self.last_of.values() if v is not None) | set(self.open_dma)
        self.open_dma = set()
        for e in ENGS:
            self.op(e, None, extra=deps)

    def emit(self):
        nc = self.nc
        ops = self.ops
        for o in ops:
            best = {}
            keep = set()
            for d in o["deps"]:
                p = ops[d]
                if p["fn"] is None:
                    continue
                if p["eng"] == "pe" and o["eng"] == "pe" and not p["dma"]:
                    continue
                if p["dma"]:
                    keep.add(d)
                else:
                    e = p["eng"]
                    if e not in best or d > best[e]:
                        best[e] = d
            keep.update(best.values())
            o["deps"] = keep
        waited_eng = {e: {p: -1 for p in ENGS} for e in ENGS}
        waited_dma = {e: set() for e in ENGS}
        for i, o in enumerate(ops):
            e = o["eng"]
            real = []
            for d in sorted(o["deps"]):
                p = ops[d]
                if p["dma"]:
                    if d in waited_dma[e]:
                        continue
                    waited_dma[e].add(d)
                    real.append(d)
                else:
                    if waited_eng[e][p["eng"]] >= d:
                        continue
                    waited_eng[e][p["eng"]] = d
                    real.append(d)
            o["waits"] = real
            for d in real:
                ops[d]["inc"] = True
        n_inc = {e: 0 for e in ENGS}
        for o in ops:
            if o["inc"] and not o["dma"]:
                n_inc[o["eng"]] += 1
        stack = contextlib.ExitStack()
        sems = {}
        for e in ENGS:
            k = (n_inc[e] + SEM_CAP - 1) // SEM_CAP
            sems[e] = [stack.enter_context(nc.semaphore(f"s_{e}_{j}")) for j in range(max(k, 1))]
        dma_sems = {e: [stack.enter_context(nc.semaphore(f"d_{e}_{j}")) for j in range(DMA_POOL)]
                    for e in ("sp", "pool", "act")}
        cnt = {e: 0 for e in ENGS}
        dma_cnt = {e: 0 for e in ENGS}
        dma_uses = {e: [0] * DMA_POOL for e in ENGS}
        per_eng = {e: [] for e in ENGS}
        for i, o in enumerate(ops):
            e = o["eng"]
            o["pre_wait"] = None
            if o["dma"]:
                j = dma_cnt[e] % DMA_POOL
                dma_cnt[e] += 1
                prev = dma_uses[e][j]
                if prev > 0:
                    o["pre_wait"] = (dma_sems[e][j], 16 * prev)
                dma_uses[e][j] = prev + 1
                o["sem"] = (dma_sems[e][j], 16 * (prev + 1))
            elif o["inc"]:
                c = cnt[e]
                cnt[e] += 1
                o["sem"] = (sems[e][c // SEM_CAP], c % SEM_CAP + 1)
            per_eng[e].append(i)

        def run_engine(e, eng):
            for i in per_eng[e]:
                o = ops[i]
                if o["pre_wait"] is not None:
                    eng.wait_ge(*o["pre_wait"])
                for d in o["waits"]:
                    s, v = ops[d]["sem"]
                    eng.wait_ge(s, v)
                if o["fn"] is None:
                    continue
                ins = o["fn"](eng)
                if o["dma"]:
                    ins.then_inc(o["sem"][0], 16)
                elif o["inc"]:
                    ins.then_inc(o["sem"][0], 1)

        with nc.Block() as block:
            @block.tensor
            def _(eng):
                run_engine("pe", eng)

            @block.scalar
            def _(eng):
                run_engine("act", eng)

            @block.vector
            def _(eng):
                run_engine("dve", eng)

            @block.gpsimd
            def _(eng):
                run_engine("pool", eng)

            @block.sync
            def _(eng):
                run_engine("sp", eng)
        stack.close()
        return {e: len(per_eng[e]) for e in ENGS}


class Arena:
    def __init__(self, nc, st, words, parent=None, base=0):
        if parent is None:
            self.t = st.enter_context(nc.sbuf_tensor("arena", [128, words], F32))
        else:
            self.t = parent.t
        self.words = base + words
        self.off = base

    def sub(self, base, words):
        return Arena(None, None, words, parent=self, base=base)

    def mark(self):
        return self.off

    def release(self, m):
        self.off = m

    def alloc(self, shape, dt):
        n = int(np.prod(shape))
        w = n if dt == F32 else (n + 1) // 2
        w = (w + 15) // 16 * 16
        assert self.off + w <= self.words, f"arena overflow {self.off}+{w}>{self.words}"
        v = self.t[:, self.off:self.off + w]
        self.off += w
        if dt != F32:
            v = v.bitcast(dt)
        v = v[:, 0:n]
        if len(shape) == 2:
            return v.rearrange("p (a b) -> p a b", a=shape[0])
        if len(shape) == 3:
            return v.rearrange("p (a b c) -> p a b c", a=shape[0], b=shape[1])
        return v


class Ctx:
    pass


def emit_norm(S, K, h, hn, blk_ids, k, col_base):
    for bi in blk_ids:
        c0, n = BLKS[bi]
        m = 0 if bi < 4 else 1
        sq = K.sq[bi % 2]
        for c in range(8):
            S.op("act", lambda e, c=c, sq=sq, c0=c0, n=n: e.activation(
                out=sq[:, c, 0:n], in_=h[:, c, c0:c0 + n], func=AF.Square),
                reads=[("h", c, bi)], writes=[("sq", bi % 2, c)])
        ps = K.ps[K.nps % 8]
        pst = ("ps", K.nps % 8)
        K.nps += 1
        for c in range(8):
            S.op("pe", lambda e, c=c, sq=sq, ps=ps, n=n: e.matmul(
                ps[:, 0:n], lhsT=K.ones_full, rhs=sq[:, c, 0:n], start=(c == 0), stop=(c == 7)),
                reads=[("sq", bi % 2, c), ("const",)], writes=[pst])
        rstd = K.rstd[bi % 2]
        S.op("act", lambda e, ps=ps, rstd=rstd, n=n: e.activation(
            out=rstd[:, 0:n], in_=ps[:, 0:n], func=AF.Sqrt, bias=K.epsc, scale=1.0),
            reads=[pst, ("const2",)], writes=[("rstd", bi % 2)])
        S.op("dve", lambda e, rstd=rstd, n=n: e.reciprocal(out=rstd[:, 0:n], in_=rstd[:, 0:n]),
            reads=[("rstd", bi % 2)], writes=[("rstd", bi % 2)])
        for c in range(8):
            tmp = K.tmp[c % 2]
            S.op("dve", lambda e, c=c, tmp=tmp, rstd=rstd, c0=c0, n=n, m=m: e.scalar_tensor_tensor(
                out=tmp[:, 0:n], in0=h[:, c, c0:c0 + n], scalar=K.mv[:, 0 * 3 + k, c, m:m + 1],
                in1=rstd[:, 0:n], op0=ALU.mult, op1=ALU.mult),
                reads=[("h", c, bi), ("rstd", bi % 2), ("mv",)], writes=[("tmp", c % 2)])
            S.op("act", lambda e, c=c, tmp=tmp, c0=c0, n=n, m=m: e.activation(
                out=hn[:, c, c0 - col_base:c0 - col_base + n], in_=tmp[:, 0:n], func=AF.Identity,
                bias=K.mv[:, 1 * 3 + k, c, m:m + 1], scale=1.0),
                reads=[("tmp", c % 2), ("mv",)], writes=[("hn", c, bi)])


def emit_ffn(S, K, h, k, wi_d, wo_d, A):
    mk = A.mark()
    for tg in TGS:
        _ffn_group(S, K, h, k, wi_d, wo_d, A, tg)
    A.release(mk)


def _ffn_group(S, K, h, k, wi_d, wo_d, A, tg):
    if True:
        col_base = BLKS[tg[0]][0]
        ncol = sum(BLKS[b][1] for b in tg)
        m2 = A.mark()
        hn = A.alloc([8, ncol], BF16)
        act = A.alloc([22, ncol], BF16)
        wib = [A.alloc([8, 256], BF16) for _ in range(2)]
        wob = [A.alloc([22, 128], BF16) for _ in range(2)]
        sg = [A.alloc([512], F32) for _ in range(2)]
        emit_norm(S, K, h, hn, tg, k, col_base)
        wiv = wi_d.rearrange("(kc p) n -> p kc n", p=128)
        for j in range(22):
            wb = wib[j % 2]
            S.op("pool", lambda e, wb=wb, j=j: e.dma_start(out=wb[:, :, 0:128], in_=wiv[:, :, j * 128:(j + 1) * 128]),
                 writes=[("wib", j % 2, 0)], dma=True)
            S.op("pool", lambda e, wb=wb, j=j: e.dma_start(out=wb[:, :, 128:256],
                                                          in_=wiv[:, :, DFF + j * 128:DFF + (j + 1) * 128]),
                 writes=[("wib", j % 2, 1)], dma=True)
            for bi in tg:
                c0, n = BLKS[bi]
                l0 = c0 - col_base
                pg = K.ps[K.nps % 8]
                pgt = ("ps", K.nps % 8)
                K.nps += 1
                pu = K.ps[K.nps % 8]
                put = ("ps", K.nps % 8)
                K.nps += 1
                for half, (pp, ppt) in enumerate(((pg, pgt), (pu, put))):
                    for kc in range(8):
                        S.op("pe", lambda e, pp=pp, wb=wb, kc=kc, half=half, l0=l0, n=n: e.matmul(
                            pp[:, 0:n], lhsT=wb[:, kc, half * 128:(half + 1) * 128], rhs=hn[:, kc, l0:l0 + n],
                            start=(kc == 0), stop=(kc == 7)),
                            reads=[("wib", j % 2, half), ("hn", kc, bi)], writes=[ppt])
                sgt = sg[K.nsg % 2]
                sgtok = ("sg", K.nsg % 2)
                K.nsg += 1
                S.op("act", lambda e, pg=pg, sgt=sgt, n=n: e.activation(out=sgt[:, 0:n], in_=pg[:, 0:n], func=AF.Silu),
                     reads=[pgt], writes=[sgtok])
                S.op("dve", lambda e, pu=pu, sgt=sgt, j=j, l0=l0, n=n: e.tensor_tensor(
                    out=act[:, j, l0:l0 + n], in0=pu[:, 0:n], in1=sgt[:, 0:n], op=ALU.mult),
                    reads=[put, sgtok], writes=[("act", j, bi)])
        if getattr(K, "dbg", None) is not None and tg[0] == 0:
            dd = K.dbg
            S.op("sp", lambda e: e.dma_start(out=dd["act"], in_=act), reads=[("act", j, b) for j in range(22) for b in tg], writes=[("dbg", 0)], dma=True)
            S.op("sp", None, reads=[("dbg", 0)])
            raise StopIteration
        wov = wo_d.rearrange("(kc p) n -> p kc n", p=128)
        for oc in range(8):
            wb = wob[oc % 2]
            S.op("pool", lambda e, wb=wb, oc=oc: e.dma_start(out=wb[:, 0:11, :], in_=wov[:, 0:11, oc * 128:(oc + 1) * 128]),
                 writes=[("wob", oc % 2, 0)], dma=True)
            S.op("pool", lambda e, wb=wb, oc=oc: e.dma_start(out=wb[:, 11:22, :], in_=wov[:, 11:22, oc * 128:(oc + 1) * 128]),
                 writes=[("wob", oc % 2, 1)], dma=True)
            for bi in tg:
                c0, n = BLKS[bi]
                l0 = c0 - col_base
                m = 0 if bi < 4 else 1
                po = K.ps[K.nps % 8]
                pot = ("ps", K.nps % 8)
                K.nps += 1
                for kc in range(22):
                    S.op("pe", lambda e, po=po, wb=wb, kc=kc, l0=l0, n=n: e.matmul(
                        po[:, 0:n], lhsT=wb[:, kc, :], rhs=act[:, kc, l0:l0 + n], start=(kc == 0), stop=(kc == 21)),
                        reads=[("wob", oc % 2, kc // 11), ("act", kc, bi)], writes=[pot])
                S.op("dve", lambda e, po=po, oc=oc, c0=c0, n=n, m=m: e.scalar_tensor_tensor(
                    out=h[:, oc, c0:c0 + n], in0=po[:, 0:n], scalar=K.mv[:, 2 * 3 + k, oc, m:m + 1],
                    in1=h[:, oc, c0:c0 + n], op0=ALU.mult, op1=ALU.add),
                    reads=[pot, ("h", oc, bi), ("mv",)], writes=[("h", oc, bi)])
        S.barrier()
        A.release(m2)


def setup_common(nc, st, S):
    K = Ctx()
    K.pbig = st.enter_context(nc.psum_tensor("pbig", [128, 4096], F32))
    K.ps = [K.pbig[:, i * 512:(i + 1) * 512] for i in range(8)]
    K.nps = 0
    K.nsg = 0
    return K


def load_consts(nc, S, K, A, consts_d):
    cst = A.alloc([3, 128], BF16)
    S.op("sp", lambda e: e.dma_start(out=cst, in_=consts_d), writes=[("const",)], dma=True)
    K.ones_full = cst[:, 0, :]
    K.ones_blk = cst[:, 1, :]
    K.perm = cst[:, 2, :]
    K.epsc = A.alloc([1], F32)
    S.op("dve", lambda e: e.memset(K.epsc, EPS), writes=[("const2",)])
    K.sq = [A.alloc([8, 512], BF16) for _ in range(2)]
    K.rstd = [A.alloc([512], F32) for _ in range(2)]
    K.tmp = [A.alloc([512], F32) for _ in range(2)]


def build_A():
    nc = bass.Bass("TRN2", target_bir_lowering=False)

    def din(name, shape, dt=F32):
        return nc.dram_tensor(name, shape, dt, kind="ExternalInput").ap()

    def dout(name, shape, dt=F32):
        return nc.dram_tensor(name, shape, dt, kind="ExternalOutput").ap()

    hin = din("hin", [D, T])
    cs_d = din("cs", [128, 8, 2])
    wada = din("w_ada", [D, 9 * D])
    bada = din("b_ada", [128, 72])
    ng_d = din("norm_g", [128, 3, 8])
    wi_d = din("wi", [D, 2 * DFF])
    wo_d = din("wo", [DFF, D])
    win_d = din("w_in", [D, INW])
    qkg_d = din("qkg", [128, 4])
    cos_d = din("cos", [128, T])
    sin_d = din("sin", [128, T])
    consts_d = din("consts", [128, 3, 128], BF16)
    cdft_d = din("cdft", [128, 256], BF16)

    hmid = dout("hmid", [D, T])
    hn_o = dout("hn", [D, T], BF16)
    mv_o = dout("mv", [128, 9, 8, 2])
    qa_o = dout("qa", [512, T], BF16)
    qb_o = dout("qb", [512, T], BF16)
    ka_o = dout("ka", [128, T], BF16)
    kb_o = dout("kb", [512, T], BF16)
    va_o = dout("va", [T, 128], BF16)
    vb_o = dout("vb", [T, 512], BF16)
    w12_o = dout("w12", [T, 4, 256], BF16)
    out_tokens = []

    S = Sched(nc)
    with contextlib.ExitStack() as st:
        K = setup_common(nc, st, S)
        A = Arena(nc, st, 52800)
        h = A.alloc([8, T], F32)
        K.mv = A.alloc([9, 8, 2], F32)
        load_consts(nc, S, K, A, consts_d)
        hv = hin.rearrange("(c p) t -> p c t", p=128)
        for c in range(8):
            S.op("sp", lambda e, c=c: e.dma_start(out=h[:, c, :], in_=hv[:, c, :]),
                 writes=[("h", c, b) for b in range(5)], dma=True)

        mk = A.mark()
        cs_t = A.alloc([8, 2], F32)
        sc = A.alloc([8, 2], F32)
        bt = A.alloc([72], F32)
        gt = A.alloc([3, 8], F32)
        mod = A.alloc([72, 2], F32)
        stg = [A.alloc([8, 512], F32) for _ in range(4)]
        S.op("sp", lambda e: e.dma_start(out=cs_t, in_=cs_d), writes=[("cs",)], dma=True)
        S.op("sp", lambda e: e.dma_start(out=bt, in_=bada), writes=[("bt",)], dma=True)
        S.op("sp", lambda e: e.dma_start(out=gt, in_=ng_d), writes=[("gt",)], dma=True)
        S.op("act", lambda e: e.activation(out=sc, in_=cs_t, func=AF.Silu), reads=[("cs",)], writes=[("sc",)])
        wav = wada.rearrange("(kc p) n -> p kc n", p=128)
        pm = K.ps[0]
        for cg in range(18):
            sb_ = stg[cg % 4]
            S.op("sp", lambda e, sb_=sb_, cg=cg: e.dma_start(out=sb_, in_=wav[:, :, cg * 512:(cg + 1) * 512]),
                 writes=[("stg", cg % 4)], dma=True)
            for j in range(4):
                ch = cg * 4 + j
                for kc in range(8):
                    S.op("pe", lambda e, sb_=sb_, j=j, kc=kc, ch=ch: e.matmul(
                        pm[:, 2 * ch:2 * ch + 2], lhsT=sb_[:, kc, j * 128:(j + 1) * 128], rhs=sc[:, kc, :],
                        start=(kc == 0), stop=(kc == 7)),
                        reads=[("stg", cg % 4), ("sc",)], writes=[("ps", 0)])
        pmv = pm[:, 0:144].rearrange("p (j m) -> p j m", m=2)
        for m in range(2):
            S.op("dve", lambda e, m=m: e.tensor_tensor(out=mod[:, :, m], in0=pmv[:, :, m], in1=bt, op=ALU.add),
                 reads=[("ps", 0), ("bt",)], writes=[("mod", m)])
        K.nps = 1
        for k in range(3):
            for m in range(2):
                S.op("dve", lambda e, k=k, m=m: e.scalar_tensor_tensor(
                    out=K.mv[:, 0 * 3 + k, :, m], in0=mod[:, (3 * k + 1) * 8:(3 * k + 2) * 8, m], scalar=1.0,
                    in1=gt[:, k, :], op0=ALU.add, op1=ALU.mult),
                    reads=[("mod", m), ("gt",)], writes=[("mv",)])
                S.op("dve", lambda e, k=k, m=m: e.tensor_copy(
                    out=K.mv[:, 1 * 3 + k, :, m], in_=mod[:, (3 * k) * 8:(3 * k + 1) * 8, m]),
                    reads=[("mod", m)], writes=[("mv",)])
                S.op("dve", lambda e, k=k, m=m: e.tensor_scalar(
                    out=K.mv[:, 2 * 3 + k, :, m], in0=mod[:, (3 * k + 2) * 8:(3 * k + 3) * 8, m],
                    scalar1=(1.0 if k == 1 else 0.5), scalar2=None, op0=ALU.mult),
                    reads=[("mod", m)], writes=[("mv",)])
        S.op("sp", lambda e: e.dma_start(out=mv_o, in_=K.mv), reads=[("mv",)], writes=[("o_mv",)], dma=True)
        out_tokens.append(("o_mv",))
        S.barrier()
        A.release(mk)

        if DEBUG:
            K.dbg = dict(act=dout("dbg_act", [128, 22, 1024], BF16))
            try:
                emit_ffn(S, K, h, 0, wi_d, wo_d, A)
            except StopIteration:
                pass
            stats = S.emit()
            return nc, stats
        emit_ffn(S, K, h, 0, wi_d, wo_d, A)

        hn = A.alloc([8, T], BF16)
        emit_norm(S, K, h, hn, range(5), 1, 0)
        hmv = hmid.rearrange("(c p) t -> p c t", p=128)
        hnv = hn_o.rearrange("(c p) t -> p c t", p=128)
        for c in range(8):
            S.op("sp", lambda e, c=c: e.dma_start(out=hmv[:, c, :], in_=h[:, c, :]),
                 reads=[("h", c, b) for b in range(5)], writes=[("o_h", c)], dma=True)
            S.op("sp", lambda e, c=c: e.dma_start(out=hnv[:, c, :], in_=hn[:, c, :]),
                 reads=[("hn", c, b) for b in range(5)], writes=[("o_hn", c)], dma=True)
            out_tokens += [("o_h", c), ("o_hn", c)]
        S.barrier()
        HA = A.sub(0, 8 * T)
        cos_t = HA.alloc([T], F32)
        sin_t = HA.alloc([T], F32)
        qkg = A.alloc([4], F32)
        qkg2 = A.alloc([4], F32)
        cdft = A.alloc([256], BF16)
        S.op("sp", lambda e: e.dma_start(out=cos_t, in_=cos_d), writes=[("cos",)], dma=True)
        S.op("sp", lambda e: e.dma_start(out=sin_t, in_=sin_d), writes=[("sin",)], dma=True)
        S.op("sp", lambda e: e.dma_start(out=qkg, in_=qkg_d), writes=[("qkg",)], dma=True)
        S.op("sp", lambda e: e.dma_start(out=cdft, in_=cdft_d), writes=[("cdft",)], dma=True)
        S.op("dve", lambda e: e.tensor_copy(out=qkg2, in_=qkg), reads=[("qkg",)], writes=[("qkg2",)])
        for gi in (0, 2):
            S.op("dve", lambda e, gi=gi: e.tensor_scalar(out=qkg2[:, gi:gi + 1], in0=qkg[:, gi:gi + 1], scalar1=0.125,
                                                       scalar2=None, op0=ALU.mult),
                 reads=[("qkg",), ("qkg2",)], writes=[("qkg2",)])
        winv = win_d.rearrange("(kc p) n -> p kc n", p=128)
        wqb = [A.alloc([8, 128], BF16) for _ in range(3)]
        NS = 4
        qsq = [A.alloc([512], BF16) for _ in range(NS)]
        pgb = [A.alloc([512], BF16) for _ in range(NS)]
        qrs = [A.alloc([512], F32) for _ in range(NS)]
        t1 = [A.alloc([512], F32) for _ in range(NS)]
        t2 = [A.alloc([512], F32) for _ in range(NS)]
        ostg = [A.alloc([T], BF16) for _ in range(2)]
        chunks = []
        for i in range(4):
            chunks.append((i * 128, 0, qa_o[i * 128:(i + 1) * 128, :]))
        chunks.append((512, 1, ka_o[:, :]))
        for i in range(4):
            chunks.append((768 + i * 128, 2, qb_o[i * 128:(i + 1) * 128, :]))
        for i in range(4):
            chunks.append((1280 + i * 128, 3, kb_o[i * 128:(i + 1) * 128, :]))
        items = []
        for ci, (col0, gi, oap) in enumerate(chunks):
            for bi in range(5):
                items.append((ci, col0, gi, oap, bi))
        PPB, PMB, PRB = (0, 1), (2, 3), (4, 5, 6)

        def qk_stage(it, sidx, stg_):
            ci, col0, gi, oap, bi = items[it]
            c0, n = BLKS[bi]
            b4 = it % NS
            wb = wqb[ci % 3]
            og = ostg[ci % 2]
            ppi, pmi, pri = PPB[it % 2], PMB[it % 2], PRB[it % 3]
            pp, pmm, pr = K.ps[ppi], K.ps[pmi], K.ps[pri]
            if stg_ == 0:
                if bi == 0:
                    S.op("pool", lambda e: e.dma_start(out=wb, in_=winv[:, :, col0:col0 + 128]),
                         writes=[("wqb", ci % 3)], dma=True)
                for kc in range(8):
                    S.op("pe", lambda e, kc=kc: e.matmul(
                        pp[:, 0:n], lhsT=wb[:, kc, :], rhs=hn[:, kc, c0:c0 + n], start=(kc == 0), stop=(kc == 7)),
                        reads=[("wqb", ci % 3), ("hn", kc, bi)], writes=[("ps", ppi)])
            elif stg_ == 1:
                S.op("act", lambda e: e.activation(out=qsq[b4][:, 0:n], in_=pp[:, 0:n], func=AF.Square),
                     reads=[("ps", ppi)], writes=[("qsq", b4)])
                S.op("act", lambda e: e.activation(out=pgb[b4][:, 0:n], in_=pp[:, 0:n], func=AF.Identity,
                                                   scale=qkg2[:, gi:gi + 1]),
                     reads=[("ps", ppi), ("qkg2",)], writes=[("pgb", b4)])
            elif stg_ == 2:
                S.op("pe", lambda e: e.matmul(pmm[:, 0:n], lhsT=K.ones_blk, rhs=qsq[b4][:, 0:n], start=True, stop=True),
                     reads=[("qsq", b4), ("const",)], writes=[("ps", pmi)])
                S.op("pe", lambda e: e.matmul(pr[:, 0:n], lhsT=K.perm, rhs=pgb[b4][:, 0:n], start=True, stop=True),
                     reads=[("pgb", b4), ("const",)], writes=[("ps", pri)])
            elif stg_ == 3:
                S.op("act", lambda e: e.activation(out=qrs[b4][:, 0:n], in_=pmm[:, 0:n], func=AF.Ln, bias=K.epsc, scale=1.0),
                     reads=[("ps", pmi), ("const2",)], writes=[("qrs", b4)])
                S.op("act", lambda e: e.activation(out=qrs[b4][:, 0:n], in_=qrs[b4][:, 0:n], func=AF.Exp, scale=-0.5),
                     reads=[("qrs", b4)], writes=[("qrs", b4)])
                S.op("pool", lambda e: e.tensor_tensor(out=t1[b4][:, 0:n], in0=pgb[b4][:, 0:n], in1=cos_t[:, c0:c0 + n], op=ALU.mult),
                     reads=[("pgb", b4), ("cos",)], writes=[("t1", b4)])
            elif stg_ == 4:
                S.op("dve", lambda e: e.tensor_tensor(out=t2[b4][:, 0:n], in0=pr[:, 0:n], in1=sin_t[:, c0:c0 + n], op=ALU.mult),
                     reads=[("ps", pri), ("sin",)], writes=[("t2", b4)])
                S.op("pool", lambda e: e.tensor_tensor(out=t1[b4][:, 0:n], in0=t1[b4][:, 0:n], in1=t2[b4][:, 0:n], op=ALU.add),
                     reads=[("t1", b4), ("t2", b4)], writes=[("t1", b4)])
            else:
                S.op("dve", lambda e: e.tensor_tensor(out=og[:, c0:c0 + n], in0=t1[b4][:, 0:n], in1=qrs[b4][:, 0:n], op=ALU.mult),
                     reads=[("t1", b4), ("qrs", b4)], writes=[("ostg", ci % 2, bi)])
                if bi == 4:
                    S.op("sp", lambda e: e.dma_start(out=oap, in_=og),
                         reads=[("ostg", ci % 2, b) for b in range(5)], writes=[("o_q", ci)], dma=True)
                    out_tokens.append(("o_q", ci))

        NST = 6
        for t in range(len(items) + NST - 1):
            for sg2 in range(NST):
                it = t - sg2
                if 0 <= it < len(items):
                    qk_stage(it, it, sg2)
        K.nps = 7

        w12stg = HA.alloc([17, 4, 256], BF16)
        for g in range(4):
            ci = 13 + g
            wb = wqb[ci % 3]
            col0 = 2304 + g * 128
            S.op("pool", lambda e, wb=wb, col0=col0: e.dma_start(out=wb, in_=winv[:, :, col0:col0 + 128]),
                 writes=[("wqb", ci % 3)], dma=True)
            ug = ostg[ci % 2]
            for bi in range(5):
                c0, n = BLKS[bi]
                pp = K.ps[K.nps % 8]; ppt = ("ps", K.nps % 8); K.nps += 1
                for kc in range(8):
                    S.op("pe", lambda e, pp=pp, wb=wb, kc=kc, c0=c0, n=n: e.matmul(
                        pp[:, 0:n], lhsT=wb[:, kc, :], rhs=hn[:, kc, c0:c0 + n], start=(kc == 0), stop=(kc == 7)),
                        reads=[("wqb", ci % 3), ("hn", kc, bi)], writes=[ppt])
                S.op("act", lambda e, pp=pp, ug=ug, c0=c0, n=n: e.activation(out=ug[:, c0:c0 + n], in_=pp[:, 0:n],
                                                                            func=AF.Identity),
                     reads=[ppt], writes=[("ostg", ci % 2, bi)])
            for tt in range(17):
                nt = 128 if tt < 16 else 64
                pp = K.ps[K.nps % 8]; ppt = ("ps", K.nps % 8); K.nps += 1
                S.op("pe", lambda e, pp=pp, ug=ug, tt=tt, nt=nt: e.matmul(
                    pp[0:nt, 0:256], lhsT=ug[:, tt * 128:tt * 128 + nt], rhs=cdft, start=True, stop=True),
                    reads=[("ostg", ci % 2, tt // 4), ("cdft",)], writes=[ppt])
                S.op("dve", lambda e, pp=pp, tt=tt, nt=nt, g=g: e.tensor_copy(out=w12stg[0:nt, tt, g, :], in_=pp[0:nt, 0:256]),
                     reads=[ppt], writes=[("w12stg", tt)])
        w12v = w12_o[0:2048].rearrange("(tt p) g c -> p tt g c", p=128)
        S.op("sp", lambda e: e.dma_start(out=w12v, in_=w12stg[:, 0:16]), reads=[("w12stg", tt) for tt in range(16)],
             writes=[("o_w12", 0)], dma=True)
        S.op("sp", lambda e: e.dma_start(out=w12_o[2048:2112], in_=w12stg[0:64, 16]), reads=[("w12stg", 16)],
             writes=[("o_w12", 1)], dma=True)
        out_tokens += [("o_w12", 0), ("o_w12", 1)]

        wva = A.alloc([8, 128], BF16)
        wvb = A.alloc([8, 512], BF16)
        vstg = A.alloc([17, 640], BF16)
        S.op("pool", lambda e: e.dma_start(out=wva, in_=winv[:, :, 640:768]), writes=[("wva",)], dma=True)
        for q4 in range(4):
            S.op("pool", lambda e, q4=q4: e.dma_start(out=wvb[:, 2 * q4:2 * q4 + 2, :], in_=winv[:, 2 * q4:2 * q4 + 2, 1792:2304]),
                 writes=[("wvb", q4)], dma=True)
        for tt in range(17):
            nt = 128 if tt < 16 else 64
            pa = K.ps[K.nps % 8]; pat = ("ps", K.nps % 8); K.nps += 1
            pb = K.ps[K.nps % 8]; pbt = ("ps", K.nps % 8); K.nps += 1
            for kc in range(8):
                S.op("pe", lambda e, pa=pa, kc=kc, tt=tt, nt=nt: e.matmul(
                    pa[0:nt, 0:128], lhsT=hn[:, kc, tt * 128:tt * 128 + nt], rhs=wva[:, kc, :], start=(kc == 0), stop=(kc == 7)),
                    reads=[("wva",), ("hn", kc, tt // 4)], writes=[pat])
            for kc in range(8):
                S.op("pe", lambda e, pb=pb, kc=kc, tt=tt, nt=nt: e.matmul(
                    pb[0:nt, 0:512], lhsT=hn[:, kc, tt * 128:tt * 128 + nt], rhs=wvb[:, kc, :], start=(kc == 0), stop=(kc == 7)),
                    reads=[("wvb", kc // 2), ("hn", kc, tt // 4)], writes=[pbt])
            S.op("dve", lambda e, pa=pa, tt=tt, nt=nt: e.tensor_copy(out=vstg[0:nt, tt, 0:128], in_=pa[0:nt, 0:128]),
                 reads=[pat], writes=[("vstg", tt, 0)])
            S.op("act", lambda e, pb=pb, tt=tt, nt=nt: e.activation(out=vstg[0:nt, tt, 128:640], in_=pb[0:nt, 0:512],
                                                                  func=AF.Identity),
                 reads=[pbt], writes=[("vstg", tt, 1)])
        vav = va_o[0:2048].rearrange("(tt p) c -> p tt c", p=128)
        vbv = vb_o[0:2048].rearrange("(tt p) c -> p tt c", p=128)
        S.op("sp", lambda e: e.dma_start(out=vav, in_=vstg[:, 0:16, 0:128]), reads=[("vstg", tt, 0) for tt in range(16)],
             writes=[("o_v", 0)], dma=True)
        S.op("sp", lambda e: e.dma_start(out=vbv, in_=vstg[:, 0:16, 128:640]), reads=[("vstg", tt, 1) for tt in range(16)],
             writes=[("o_v", 1)], dma=True)
        S.op("sp", lambda e: e.dma_start(out=va_o[2048:2112], in_=vstg[0:64, 16, 0:128]), reads=[("vstg", 16, 0)],
             writes=[("o_v", 2)], dma=True)
        S.op("sp", lambda e: e.dma_start(out=vb_o[2048:2112], in_=vstg[0:64, 16, 128:640]), reads=[("vstg", 16, 1)],
             writes=[("o_v", 3)], dma=True)
        out_tokens += [("o_v", i) for i in range(4)]
        S.op("sp", None, reads=out_tokens)
        stats = S.emit()
    return nc, stats


def _bf(a):
    return np.asarray(a, dtype=np.float32).astype(NPBF)


def host_consts():
    k = np.arange(128)
    ones_full = np.full((128, 128), 1.0 / 1024, np.float32)
    ones_blk = ((k[:, None] // 64) == (k[None, :] // 64)).astype(np.float32) / 64.0
    perm = np.zeros((128, 128), np.float32)
    for m in range(128):
        if m % 64 < 32:
            perm[m + 32, m] = -1.0
        else:
            perm[m - 32, m] = 1.0
    consts = np.stack([ones_full, ones_blk, perm], axis=1)
    ang = 2.0 * np.pi * ((k[:, None] * k[None, :]) % 128) / 128.0
    cdft = np.concatenate([np.cos(ang), -np.sin(ang)], axis=1) / np.sqrt(128.0)
    return _bf(consts), _bf(cdft)


def host_rope(core):
    j = core % 4
    n = np.arange(j * TL, (j + 1) * TL)
    row = (n // 64).astype(np.float32)
    col = (n % 64).astype(np.float32)
    inv = np.power(np.float32(10000.0), -np.arange(16, dtype=np.float32) / np.float32(16)).astype(np.float32)
    ang = np.concatenate([row[:, None] * inv, col[:, None] * inv], axis=-1).astype(np.float32)
    cos = np.ones((128, T), np.float32)
    sin = np.zeros((128, T), np.float32)
    p = np.arange(128) % 32
    cos[:, :TL] = np.cos(ang).astype(np.float32).T[p]
    sin[:, :TL] = np.sin(ang).astype(np.float32).T[p]
    return cos, sin


def prep_A(inp, l, hin_cores):
    consts, cdft = host_consts()
    maps = []
    for r in range(NCORE):
        b = r // 4
        cs = np.stack([inp["c"][b].reshape(8, 128).T, inp["c_ctx"].reshape(8, 128).T], axis=-1)
        cos, sin = host_rope(r)
        maps.append({
            "hin": np.ascontiguousarray(hin_cores[r], dtype=np.float32),
            "cs": np.ascontiguousarray(cs, dtype=np.float32),
            "w_ada": inp["w_ada"][l],
            "b_ada": np.ascontiguousarray(inp["b_ada"][l].reshape(72, 128).T),
            "norm_g": np.ascontiguousarray(inp["norm_g"][l].reshape(3, 8, 128).transpose(2, 0, 1)),
            "wi": inp["ffn_wi"][l, 0],
            "wo": inp["ffn_wo"][l, 0],
            "w_in": inp["w_in"][l],
            "qkg": np.ascontiguousarray(np.tile(inp["qk_g"][l].T, (2, 1))),
            "cos": cos, "sin": sin, "consts": consts, "cdft": cdft,
        })
    return maps


def first_hin(inp):
    out = []
    for r in range(NCORE):
        b, j = r // 4, r % 4
        hx = inp["x"][b, j * TL:(j + 1) * TL, :].T
        hc = inp["ctx"][b, j * TC:(j + 1) * TC, :].T
        out.append(np.ascontiguousarray(np.concatenate([hx, hc], axis=1)))
    return out


def build_B():
    nc = bass.Bass("TRN2", target_bir_lowering=False)

    def din(name, shape, dt=F32):
        return nc.dram_tensor(name, shape, dt, kind="ExternalInput").ap()

    def dout(name, shape, dt=F32):
        return nc.dram_tensor(name, shape, dt, kind="ExternalOutput").ap()

    hmid = din("hmid", [D, T])
    hn_d = din("hn", [D, T], BF16)
    mv_d = din("mv", [128, 9, 8, 2])
    qa_d = din("qa", [512, T], BF16)
    qb_d = din("qb", [512, T], BF16)
    ka_d = din("ka_all", [128, NKEY], BF16)
    kb_d = din("kb_all", [512, NKEY], BF16)
    va_d = din("va_all", [NKEY, 128], BF16)
    vb_d = din("vb_all", [NKEY, 512], BF16)
    w12_d = din("w12_all", [NKEY, 4, 256], BF16)
    dftL = din("dftL", [SEQ // 2 + 128, 2, TL], BF16)
    w12r_d = din("w12_rev", [SEQ // 2, 4, 256], BF16)
    dftC = din("dftC", [CTX, 2, TC], BF16)
    win_d = din("w_in", [D, INW])
    wbr_d = din("w_branch", [3, 512, D])
    wout_d = din("w_out", [D, D])
    wi_d = din("wi", [D, 2 * DFF])
    wo_d = din("wo", [DFF, D])
    dlam_d = din("dlam", [256])
    subg_d = din("subg", [128, 1])
    lconst_d = din("lconst", [128, 2])
    consts_d = din("consts", [128, 3, 128], BF16)
    consts2_d = din("consts2", [128, 2, 128], BF16)
    swap_d = din("swap", [128, 128])
    hout = dout("hout", [D, T])

    S = Sched(nc)
    with contextlib.ExitStack() as st:
        K = setup_common(nc, st, S)
        A = Arena(nc, st, 52800)
        h = A.alloc([8, T], F32)
        HA = A.sub(0, 8 * T)
        K.mv = A.alloc([9, 8, 2], F32)
        load_consts(nc, S, K, A, consts_d)
        S.op("sp", lambda e: e.dma_start(out=K.mv, in_=mv_d), writes=[("mv",)], dma=True)
        mk0 = A.mark()
        c2 = A.alloc([2, 128], BF16)
        S.op("sp", lambda e: e.dma_start(out=c2, in_=consts2_d), writes=[("c2",)], dma=True)
        ones1 = c2[:, 0, :]
        ones128 = c2[:, 1, :]
        swp = A.alloc([128], F32)
        S.op("sp", lambda e: e.dma_start(out=swp, in_=swap_d), writes=[("swp",)], dma=True)
        br = A.alloc([12, T], BF16)

        dl = A.alloc([256], F32)
        lc = A.alloc([2], F32)
        sg_ = A.alloc([1], F32)
        lw = A.alloc([8], F32)
        S.op("sp", lambda e: e.dma_start(out=dl, in_=dlam_d.partition_broadcast(128)), writes=[("dl",)], dma=True)
        S.op("sp", lambda e: e.dma_start(out=lc, in_=lconst_d), writes=[("lc",)], dma=True)
        S.op("sp", lambda e: e.dma_start(out=sg_, in_=subg_d), writes=[("subg",)], dma=True)
        prod = A.alloc([128], F32)
        S.op("dve", lambda e: e.tensor_tensor(out=prod[:, 0:64], in0=dl[:, 0:64], in1=dl[:, 64:128], op=ALU.mult),
             reads=[("dl",)], writes=[("prod", 0)])
        S.op("dve", lambda e: e.tensor_tensor(out=prod[:, 64:128], in0=dl[:, 128:192], in1=dl[:, 192:256], op=ALU.mult),
             reads=[("dl",)], writes=[("prod", 1)])
        for i in range(2):
            S.op("dve", lambda e, i=i: e.reduce_sum(out=lw[:, 4 + i:5 + i], in_=prod[:, i * 64:(i + 1) * 64],
                                                   axis=mybir.AxisListType.X),
                 reads=[("prod", i)], writes=[("lw", 4 + i)])
            S.op("act", lambda e, i=i: e.activation(out=lw[:, i:i + 1], in_=lw[:, 4 + i:5 + i], func=AF.Exp),
                 reads=[("lw", 4 + i)], writes=[("lw", i)])
        S.op("dve", lambda e: e.tensor_tensor(out=lw[:, 6:7], in0=lw[:, 1:2], in1=lw[:, 0:1], op=ALU.subtract),
             reads=[("lw", 0), ("lw", 1)], writes=[("lw", 6)])
        S.op("dve", lambda e: e.tensor_tensor(out=lw[:, 2:3], in0=lw[:, 6:7], in1=lc[:, 0:1], op=ALU.subtract),
             reads=[("lw", 6), ("lc",)], writes=[("lw", 2)])
        S.op("dve", lambda e: e.tensor_tensor(out=lw[:, 3:4], in0=sg_, in1=lc[:, 1:2], op=ALU.mult),
             reads=[("subg",), ("lc",)], writes=[("lw", 3)])

        mk = A.mark()
        qt = [A.alloc([T], BF16) for _ in range(2)]
        vaug = A.alloc([66, 192], BF16)
        Eb = [A.alloc([2, 512], BF16) for _ in range(3)]
        Osb = [A.alloc([512], F32) for _ in range(4)]
        Osb2 = [A.alloc([512], F32) for _ in range(2)]
        sqb = A.alloc([512], BF16)
        Es = [[A.alloc([512], F32) for _ in range(2)] for _ in range(2)]
        onesf = A.alloc([128], F32)
        S.op("pool", lambda e: e.memset(onesf, 1.0), writes=[("onesf",)])
        kbuf = [HA.alloc([NKEY], BF16) for _ in range(2)]
        vbuf = [HA.alloc([66, 128], BF16) for _ in range(2)]
        SB = (0, 1, 2)
        AB = (3, 4, 5, 6)
        MB = 7
        cnt = dict(s=0, e=0, a=0, o=0, q=0, k=0, v=0)
        S.op("pool", lambda e: e.memset(vaug[:, :, 0:64], 1.0), writes=[("vaug", "o1")])
        S.op("pool", lambda e: e.memset(vaug[:, :, 128:192], 1.0), writes=[("vaug", "o2")])

        def attend(qtile, q_tok, p0, kt_src, k_tok, pv_list, qblocks):
            pass

        LAG = 2
        steps = []
        pend = []

        def run_steps():
            nstep = len(steps)
            for i in range(nstep + LAG):
                if i < nstep:
                    stp = steps[i]
                    sp_ = i % 2
                    ei = i % 3
                    n = stp["n"]
                    for hf in range(2):
                        bk = 2 * sp_ + hf
                        S.op("pe", (lambda e, stp=stp, bk=bk, hf=hf: stp["qk"][hf](e, K.ps[bk])),
                             reads=stp["qk_reads"][hf], writes=[("ps", bk)])
                    spair = K.pbig[:, sp_ * 1024:(sp_ + 1) * 1024].rearrange("p (b c) -> p b c", b=2)
                    S.op("act", lambda e, spair=spair, ei=ei, n=n: e.activation(out=Eb[ei][:, :, 0:n], in_=spair[:, :, 0:n], func=AF.Exp),
                         reads=[("ps", 2 * sp_), ("ps", 2 * sp_ + 1)], writes=[("E", ei)])
                j = i - LAG
                if j >= 0:
                    stp = steps[j]
                    ei = j % 3
                    for (fn, hf, rd, wr) in stp["pv"]:
                        S.op("pe", (lambda e, fn=fn, ei=ei, hf=hf: fn(e, Eb[ei][:, hf, :])), reads=rd + [("E", ei)], writes=wr)
                    for (fn, hf, rd, wr) in stp.get("dve", ()):
                        S.op("dve", (lambda e, fn=fn, ei=ei, hf=hf: fn(e, Eb[ei][:, hf, :])), reads=rd + [("E", ei)], writes=wr)
                    if stp["post"] is not None:
                        stp["post"]()
                for pd in pend:
                    pd[0] -= 1
                for pd in [p_ for p_ in pend if p_[0] <= 0]:
                    pd[1]()
                    pend.remove(pd)
            for pd in list(pend):
                pd[1]()
            pend.clear()
            steps.clear()

        kav = ka_d
        vav = va_d.rearrange("(kt p) c -> p kt c", p=128)
        for c in range(4):
            run_steps()
            hk = c // 2
            qi = cnt["q"] % 2
            cnt["q"] += 1
            qtl = qt[qi]
            S.op("sp", lambda e, qtl=qtl, c=c: e.dma_start(out=qtl, in_=qa_d[c * 128:(c + 1) * 128, :]),
                 writes=[("qt", qi)], dma=True)
            if c % 2 == 0:
                ki = cnt["k"] % 2
                cnt["k"] += 1
                kb_ = kbuf[ki]
                for half in range(2):
                    S.op("sp", lambda e, kb_=kb_, half=half, hk=hk: e.dma_start(
                        out=kb_[half * 64:(half + 1) * 64, :], in_=kav[hk * 64:(hk + 1) * 64, :]),
                        writes=[("kbuf", ki, half)], dma=True)
                S.op("sp", lambda e, hk=hk: e.dma_start(out=vaug[:, :, 64:128], in_=vav[:, :, hk * 64:(hk + 1) * 64]),
                     writes=[("vaug", "v")], dma=True)
            for bi in range(5):
                c0, n = BLKS[bi]
                nkt = 66 if bi < 4 else 2

                def post_a(c=c, c0=c0, n=n, bi=bi):
                    for hh in range(2):
                        p0 = hh * 64
                        s0 = 64 - p0
                        ab = 4 + hh
                        acc = K.ps[ab]
                        oi = cnt["o"] % 4
                        cnt["o"] += 1
                        ob = Osb[oi]
                        mb = 6 + oi % 2
                        S.op("act", lambda e, ob=ob, acc=acc: e.activation(out=ob[:, 0:n], in_=acc[:, 0:n], func=AF.Identity),
                             reads=[("ps", ab)], writes=[("osb", oi)])
                        S.op("dve", lambda e, ob=ob, s0=s0: e.reciprocal(out=ob[s0:s0 + 64, 0:n], in_=ob[s0:s0 + 64, 0:n]),
                             reads=[("osb", oi)], writes=[("osb", oi)])

                        def fin(ob=ob, oi=oi, mb=mb, p0=p0, hh=hh):
                            S.op("pe", lambda e: e.matmul(K.ps[mb][:, 0:n], lhsT=swp, rhs=ob[:, 0:n], start=True, stop=True),
                                 reads=[("osb", oi), ("swp",)], writes=[("ps", mb)])
                            S.op("dve", lambda e: e.tensor_tensor(
                                out=br[p0:p0 + 64, c, c0:c0 + n], in0=K.ps[mb][p0:p0 + 64, 0:n], in1=ob[p0:p0 + 64, 0:n], op=ALU.mult),
                                reads=[("ps", mb), ("osb", oi)], writes=[("br", c, bi, hh)])
                        pend.append([3, fin])

                for kt in range(nkt):
                    qk = []
                    qkr = []
                    pv = []
                    for hh in range(2):
                        p0 = hh * 64
                        vsl = (64, 192) if hh == 0 else (0, 128)
                        qk.append(lambda e, sps, kb_=kb_, qtl=qtl, kt=kt, p0=p0, c0=c0, n=n: e.matmul(
                            sps[:, 0:n], lhsT=kb_[p0:p0 + 64, kt * 128:(kt + 1) * 128], rhs=qtl[p0:p0 + 64, c0:c0 + n],
                            start=True, stop=True))
                        qkr.append([("kbuf", ki, hh), ("qt", qi)])
                        pv.append(((lambda e, Et, hh=hh, kt=kt, vsl=vsl, n=n, nkt=nkt: e.matmul(
                            K.ps[4 + hh][:, 0:n], lhsT=vaug[:, kt, vsl[0]:vsl[1]], rhs=Et[:, 0:n],
                            start=(kt == 0), stop=(kt == nkt - 1))),
                            hh, [("vaug", "v"), ("vaug", "o1"), ("vaug", "o2")], [("ps", 4 + hh)]))
                    steps.append(dict(n=n, qk=qk, qk_reads=qkr, pv=pv, post=(post_a if kt == nkt - 1 else None)))
        run_steps()

        vbv = vb_d.rearrange("(kt p) c -> p kt c", p=128)
        for hb in range(4):
            run_steps()
            qi = cnt["q"] % 2
            cnt["q"] += 1
            qtl = qt[qi]
            S.op("sp", lambda e, qtl=qtl, hb=hb: e.dma_start(out=qtl, in_=qb_d[hb * 128:(hb + 1) * 128, :]),
                 writes=[("qt", qi)], dma=True)
            ki = cnt["k"] % 2
            cnt["k"] += 1
            kb_ = kbuf[ki]
            S.op("sp", lambda e, kb_=kb_, hb=hb: e.dma_start(out=kb_, in_=kb_d[hb * 128:(hb + 1) * 128, :]),
                 writes=[("kbuf", ki, 0), ("kbuf", ki, 1)], dma=True)
            vi = cnt["v"] % 2
            cnt["v"] += 1
            vb_ = vbuf[vi]
            S.op("sp", lambda e, vb_=vb_, hb=hb: e.dma_start(out=vb_, in_=vbv[:, :, hb * 128:(hb + 1) * 128]),
                 writes=[("vbuf", vi)], dma=True)
            for bi in range(5):
                c0, n = BLKS[bi]
                nkt = 66 if bi < 4 else 2

                def post_b(hb=hb, c0=c0, n=n, bi=bi):
                    S.op("dve", lambda e: e.tensor_copy(out=Osb2[0][:, 0:n], in_=K.ps[4][:, 0:n]),
                         reads=[("ps", 4)], writes=[("osb2", 0)])
                    S.op("dve", lambda e: e.tensor_copy(out=Osb2[1][:, 0:n], in_=K.ps[6][:, 0:n]),
                         reads=[("ps", 6)], writes=[("osb2", 1)])
                    S.op("dve", lambda e: e.tensor_copy(out=Osb[1][:, 0:n], in_=K.ps[7][:, 0:n]),
                         reads=[("ps", 7)], writes=[("osb", 1)])
                    S.op("pe", lambda e: e.matmul(K.ps[5][:, 0:n], lhsT=onesf, rhs=Es[bi % 2][0][:, 0:n], start=True, stop=True),
                         reads=[("es", bi % 2, 0), ("onesf",)], writes=[("ps", 5)])
                    S.op("dve", lambda e: e.reciprocal(out=Osb[0][:, 0:n], in_=K.ps[5][:, 0:n]),
                         reads=[("ps", 5)], writes=[("osb", 0)])
                    S.op("dve", lambda e: e.reciprocal(out=Osb[1][:, 0:n], in_=Osb[1][:, 0:n]),
                         reads=[("osb", 1)], writes=[("osb", 1)])
                    for mp in range(2):
                        S.op("dve", lambda e, mp=mp: e.tensor_tensor(out=Osb2[mp][:, 0:n], in0=Osb2[mp][:, 0:n],
                                                                    in1=Osb[mp][:, 0:n], op=ALU.mult),
                             reads=[("osb2", mp), ("osb", mp)], writes=[("osb2", mp)])
                    S.op("dve", lambda e: e.scalar_tensor_tensor(out=Osb[0][:, 0:n], in0=Osb2[1][:, 0:n], scalar=lw[:, 2:3],
                                                                in1=Osb2[0][:, 0:n], op0=ALU.mult, op1=ALU.add),
                         reads=[("osb2", 0), ("osb2", 1), ("lw", 2), ("osb", 0)], writes=[("osb", 0)])
                    S.op("act", lambda e: e.activation(out=sqb[:, 0:n], in_=Osb[0][:, 0:n], func=AF.Square),
                         reads=[("osb", 0)], writes=[("sqb",)])
                    S.op("pe", lambda e: e.matmul(K.ps[5][:, 0:n], lhsT=ones128, rhs=sqb[:, 0:n], start=True, stop=True),
                         reads=[("sqb",), ("c2",)], writes=[("ps", 5)])
                    S.op("act", lambda e: e.activation(out=Osb[1][:, 0:n], in_=K.ps[5][:, 0:n], func=AF.Sqrt, bias=K.epsc, scale=1.0),
                         reads=[("ps", 5), ("const2",)], writes=[("osb", 1)])
                    S.op("dve", lambda e: e.reciprocal(out=Osb[1][:, 0:n], in_=Osb[1][:, 0:n]),
                         reads=[("osb", 1)], writes=[("osb", 1)])
                    S.op("dve", lambda e: e.scalar_tensor_tensor(
                        out=br[:, 4 + hb, c0:c0 + n], in0=Osb[0][:, 0:n], scalar=lw[:, 3:4], in1=Osb[1][:, 0:n],
                        op0=ALU.mult, op1=ALU.mult),
                        reads=[("osb", 0), ("osb", 1), ("lw", 3)], writes=[("br", 4 + hb, bi, 0), ("br", 4 + hb, bi, 1)])

                for kt in range(nkt):
                    qk = []
                    qkr = []
                    pv = []
                    sm = []
                    for mp in range(2):
                        p0 = mp * 64
                        qk.append(lambda e, sps, kb_=kb_, qtl=qtl, kt=kt, p0=p0, c0=c0, n=n: e.matmul(
                            sps[:, 0:n], lhsT=kb_[p0:p0 + 64, kt * 128:(kt + 1) * 128], rhs=qtl[p0:p0 + 64, c0:c0 + n],
                            start=True, stop=True))
                        qkr.append([("kbuf", ki, mp), ("qt", qi)])
                        pv.append(((lambda e, Et, kt=kt, vb_=vb_, mp=mp, n=n, nkt=nkt: e.matmul(
                            K.ps[4 + 2 * mp][:, 0:n], lhsT=vb_[:, kt, :], rhs=Et[:, 0:n],
                            start=(kt == 0), stop=(kt == nkt - 1))), mp, [("vbuf", vi)], [("ps", 4 + 2 * mp)]))
                        est = Es[bi % 2][mp]
                        if mp == 1:
                            pv.append(((lambda e, Et, kt=kt, mp=mp, n=n, nkt=nkt: e.matmul(
                                K.ps[5 + 2 * mp][:, 0:n], lhsT=ones1, rhs=Et[:, 0:n],
                                start=(kt == 0), stop=(kt == nkt - 1))), mp, [("c2",)], [("ps", 5 + 2 * mp)]))
                        elif kt == 0:
                            sm.append(((lambda e, Et, est=est, n=n: e.tensor_copy(out=est[:, 0:n], in_=Et[:, 0:n])),
                                       mp, [], [("es", bi % 2, mp)]))
                        else:
                            sm.append(((lambda e, Et, est=est, n=n: e.tensor_tensor(out=est[:, 0:n], in0=Et[:, 0:n], in1=est[:, 0:n], op=ALU.add)),
                                       mp, [("es", bi % 2, mp)], [("es", bi % 2, mp)]))
                    steps.append(dict(n=n, qk=qk, qk_reads=qkr, pv=pv, dve=sm, post=(post_b if kt == nkt - 1 else None)))
        run_steps()
        S.barrier()
        A.release(mk)

        mk = A.mark()
        w12h = HA.sub(0, 8 * T).alloc([32, 4, 256], BF16)
        dtile = [A.alloc([2, 512], BF16) for _ in range(4)]
        pa = [A.alloc([4, 4, 256], BF16) for _ in range(2)]
        pb = [A.alloc([4, 4, 256], BF16) for _ in range(2)]
        x32 = A.alloc([4, 256], BF16)
        w12v = w12_d[CTX:NKEY].rearrange("(t p) g c -> p t g c", p=128)
        w12rv = w12r_d.rearrange("(t p) g c -> p t g c", p=128)
        for pc in range(8):
            a_, b_ = pa[pc % 2], pb[pc % 2]
            S.op("sp", lambda e, a_=a_, pc=pc: e.dma_start(out=a_, in_=w12v[:, pc * 4:(pc + 1) * 4]),
                 writes=[("pa", pc % 2)], dma=True)
            S.op("sp", lambda e, b_=b_, pc=pc: e.dma_start(out=b_, in_=w12rv[:, pc * 4:(pc + 1) * 4]),
                 writes=[("pb", pc % 2)], dma=True)
            S.op("dve", lambda e, a_=a_, b_=b_, pc=pc: e.tensor_tensor(
                out=w12h[:, pc * 4:(pc + 1) * 4, :, 0:128], in0=a_[:, :, :, 0:128], in1=b_[:, :, :, 0:128], op=ALU.add),
                reads=[("pa", pc % 2), ("pb", pc % 2)], writes=[("w12h", pc, 0)])
            S.op("pool", lambda e, a_=a_, b_=b_, pc=pc: e.tensor_tensor(
                out=w12h[:, pc * 4:(pc + 1) * 4, :, 128:256], in0=a_[:, :, :, 128:256], in1=b_[:, :, :, 128:256], op=ALU.subtract),
                reads=[("pa", pc % 2), ("pb", pc % 2)], writes=[("w12h", pc, 1)])
        S.op("sp", lambda e: e.dma_start(out=x32, in_=w12v[:, 32]), writes=[("x32",)], dma=True)
        nd = 0
        for qb_ in range(4):
            c0, n = BLKS[qb_]
            for t in range(33):
                di = nd % 4
                nd += 1
                S.op("sp", lambda e, di=di, t=t, c0=c0: e.dma_start(
                    out=dtile[di], in_=dftL[t * 128:(t + 1) * 128, :, c0:c0 + 512]),
                    writes=[("dtile", di)], dma=True)
                for g in range(4):
                    for cs_ in range(2 if t < 32 else 1):
                        lh = (w12h[:, t, g, cs_ * 128:(cs_ + 1) * 128] if t < 32 else x32[:, g, 0:128])
                        rd = ([("w12h", t // 4, cs_)] if t < 32 else [("x32",)])
                        S.op("pe", lambda e, g=g, cs_=cs_, di=di, lh=lh, t=t: e.matmul(
                            K.ps[AB[g]][:, 0:512], lhsT=lh, rhs=dtile[di][:, cs_, :],
                            start=(t == 0 and cs_ == 0), stop=(t == 32)),
                            reads=rd + [("dtile", di)], writes=[("ps", AB[g])])
            for g in range(4):
                S.op("act" if g % 2 else "dve",
                     (lambda e, g=g, c0=c0: e.activation(out=br[:, 8 + g, c0:c0 + 512], in_=K.ps[AB[g]][:, 0:512], func=AF.Identity))
                     if g % 2 else
                     (lambda e, g=g, c0=c0: e.tensor_copy(out=br[:, 8 + g, c0:c0 + 512], in_=K.ps[AB[g]][:, 0:512])),
                     reads=[("ps", AB[g])], writes=[("br", 8 + g, qb_, 0), ("br", 8 + g, qb_, 1)])
        w12c = w12_d[0:CTX].rearrange("(t p) g c -> p t g c", p=128)
        dct = A.alloc([2, 2, TC], BF16)
        w12cx = A.alloc([2, 4, 256], BF16)
        S.op("sp", lambda e: e.dma_start(out=w12cx, in_=w12c), writes=[("w12cx",)], dma=True)
        S.op("sp", lambda e: e.dma_start(out=dct, in_=dftC.rearrange("(t p) s k -> p t s k", p=128)),
             writes=[("dct",)], dma=True)
        for g in range(4):
            for t in range(2):
                for cs_ in range(2):
                    S.op("pe", lambda e, g=g, cs_=cs_, t=t: e.matmul(
                        K.ps[AB[g]][:, 0:TC], lhsT=w12cx[:, t, g, cs_ * 128:(cs_ + 1) * 128], rhs=dct[:, t, cs_, :],
                        start=(t == 0 and cs_ == 0), stop=(t == 1 and cs_ == 1)),
                        reads=[("w12cx",), ("dct",)], writes=[("ps", AB[g])])
            S.op("dve", lambda e, g=g: e.tensor_copy(out=br[:, 8 + g, TL:T], in_=K.ps[AB[g]][:, 0:TC]),
                 reads=[("ps", AB[g])], writes=[("br", 8 + g, 4, 0), ("br", 8 + g, 4, 1)])
        S.barrier()
        A.release(mk)

        hv = hmid.rearrange("(c p) t -> p c t", p=128)
        for c in range(8):
            S.op("sp", lambda e, c=c: e.dma_start(out=h[:, c, :], in_=hv[:, c, :]),
                 writes=[("h", c, b) for b in range(5)], dma=True)
        K.nps = 0
        hnv = hn_d.rearrange("(c p) t -> p c t", p=128)
        winv = win_d.rearrange("(kc p) n -> p kc n", p=128)
        wbv = wbr_d.rearrange("i (ec p) d -> p i ec d", p=128)
        woutv = wout_d.rearrange("(kc p) n -> p kc n", p=128)
        for tg in TGS:
            _merge_group(S, K, A, h, br, tg, hnv, winv, wbv, woutv)

        A.release(mk0)
        emit_ffn(S, K, h, 2, wi_d, wo_d, A)
        hov = hout.rearrange("(c p) t -> p c t", p=128)
        outs = []
        for c in range(8):
            S.op("sp", lambda e, c=c: e.dma_start(out=hov[:, c, :], in_=h[:, c, :]),
                 reads=[("h", c, b) for b in range(5)], writes=[("o_h", c)], dma=True)
            outs.append(("o_h", c))
        S.op("sp", None, reads=outs)
        stats = S.emit()
    return nc, stats


def _merge_group(S, K, A, h, br, tg, hnv, winv, wbv, woutv):
    col_base = BLKS[tg[0]][0]
    ncol = sum(BLKS[b][1] for b in tg)
    m2 = A.mark()
    hng = A.alloc([8, ncol], BF16)
    mrg = A.alloc([8, ncol], BF16)
    acc = A.alloc([ncol], F32)
    gw = [A.alloc([8, 128], BF16) for _ in range(2)]
    bw = [A.alloc([4, 128], BF16) for _ in range(2)]
    ow = [A.alloc([8, 128], BF16) for _ in range(2)]
    sig = [A.alloc([512], F32) for _ in range(2)]
    tmpm = [A.alloc([512], F32) for _ in range(2)]
    for c in range(8):
        S.op("sp", lambda e, c=c: e.dma_start(out=hng[:, c, :], in_=hnv[:, c, col_base:col_base + ncol]),
             writes=[("hng", c)], dma=True)
    nw = 0
    ns = 0
    for dc in range(8):
        for i in range(3):
            wi_ = nw % 2
            nw += 1
            gcol = 2816 + i * 1024 + dc * 128
            S.op("pool", lambda e, wi_=wi_, gcol=gcol: e.dma_start(out=gw[wi_], in_=winv[:, :, gcol:gcol + 128]),
                 writes=[("gw", wi_)], dma=True)
            S.op("pool", lambda e, wi_=wi_, i=i, dc=dc: e.dma_start(out=bw[wi_], in_=wbv[:, i, :, dc * 128:(dc + 1) * 128]),
                 writes=[("bw", wi_)], dma=True)
            for bi in tg:
                c0, n = BLKS[bi]
                l0 = c0 - col_base
                pg = K.ps[K.nps % 8]; pgt = ("ps", K.nps % 8); K.nps += 1
                pp = K.ps[K.nps % 8]; ppt = ("ps", K.nps % 8); K.nps += 1
                si = ns % 2
                ns += 1
                for kc in range(8):
                    S.op("pe", lambda e, pg=pg, wi_=wi_, kc=kc, l0=l0, n=n: e.matmul(
                        pg[:, 0:n], lhsT=gw[wi_][:, kc, :], rhs=hng[:, kc, l0:l0 + n], start=(kc == 0), stop=(kc == 7)),
                        reads=[("gw", wi_), ("hng", kc)], writes=[pgt])
                for ec in range(4):
                    S.op("pe", lambda e, pp=pp, wi_=wi_, ec=ec, i=i, c0=c0, n=n: e.matmul(
                        pp[:, 0:n], lhsT=bw[wi_][:, ec, :], rhs=br[:, 4 * i + ec, c0:c0 + n], start=(ec == 0), stop=(ec == 3)),
                        reads=[("bw", wi_), ("br", 4 * i + ec, bi, 0), ("br", 4 * i + ec, bi, 1)], writes=[ppt])
                S.op("act", lambda e, pg=pg, si=si, n=n: e.activation(out=sig[si][:, 0:n], in_=pg[:, 0:n], func=AF.Sigmoid),
                     reads=[pgt], writes=[("sig", si)])
                if i == 0:
                    S.op("dve", lambda e, pp=pp, si=si, l0=l0, n=n: e.tensor_tensor(
                        out=acc[:, l0:l0 + n], in0=pp[:, 0:n], in1=sig[si][:, 0:n], op=ALU.mult),
                        reads=[ppt, ("sig", si)], writes=[("acc", bi)])
                else:
                    S.op("dve", lambda e, pp=pp, si=si, n=n: e.tensor_tensor(
                        out=tmpm[si][:, 0:n], in0=pp[:, 0:n], in1=sig[si][:, 0:n], op=ALU.mult),
                        reads=[ppt, ("sig", si)], writes=[("tmpm", si)])
                    if i == 1:
                        S.op("dve", lambda e, si=si, l0=l0, n=n: e.tensor_tensor(
                            out=acc[:, l0:l0 + n], in0=acc[:, l0:l0 + n], in1=tmpm[si][:, 0:n], op=ALU.add),
                            reads=[("acc", bi), ("tmpm", si)], writes=[("acc", bi)])
                    else:
                        S.op("dve", lambda e, si=si, l0=l0, n=n, dc=dc: e.tensor_tensor(
                            out=mrg[:, dc, l0:l0 + n], in0=acc[:, l0:l0 + n], in1=tmpm[si][:, 0:n], op=ALU.add),
                            reads=[("acc", bi), ("tmpm", si)], writes=[("mrg", dc, bi)])
    for oc in range(8):
        wi_ = oc % 2
        S.op("pool", lambda e, wi_=wi_, oc=oc: e.dma_start(out=ow[wi_], in_=woutv[:, :, oc * 128:(oc + 1) * 128]),
             writes=[("ow", wi_)], dma=True)
        for bi in tg:
            c0, n = BLKS[bi]
            l0 = c0 - col_base
            m = 0 if bi < 4 else 1
            po = K.ps[K.nps % 8]; pot = ("ps", K.nps % 8); K.nps += 1
            for kc in range(8):
                S.op("pe", lambda e, po=po, wi_=wi_, kc=kc, l0=l0, n=n: e.matmul(
                    po[:, 0:n], lhsT=ow[wi_][:, kc, :], rhs=mrg[:, kc, l0:l0 + n], start=(kc == 0), stop=(kc == 7)),
                    reads=[("ow", wi_), ("mrg", kc, bi)], writes=[pot])
            S.op("dve", lambda e, po=po, oc=oc, c0=c0, n=n, m=m: e.scalar_tensor_tensor(
                out=h[:, oc, c0:c0 + n], in0=po[:, 0:n], scalar=K.mv[:, 2 * 3 + 1, oc, m:m + 1],
                in1=h[:, oc, c0:c0 + n], op0=ALU.mult, op1=ALU.add),
                reads=[pot, ("h", oc, bi), ("mv",)], writes=[("h", oc, bi)])
    S.barrier()
    A.release(m2)


_CACHE = {}


def host_dft(j):
    key = ("dft", j)
    if key not in _CACHE:
        NH = SEQ // 2
        n = np.arange(NH + 128, dtype=np.int64)[:, None]
        k = (j * TL + np.arange(TL, dtype=np.int64))[None, :]
        ang = (2.0 * np.pi / SEQ) * ((n * k) % SEQ).astype(np.float64)
        sc = 1.0 / math.sqrt(SEQ)
        cs_tab = np.cos(ang) * sc
        sn_tab = np.sin(ang) * sc
        cs_tab[0] *= 0.5
        cs_tab[NH + 1:] = 0.0
        sn_tab[NH:] = 0.0
        L = np.stack([cs_tab, sn_tab], axis=1).astype(np.float32).astype(NPBF)
        n = np.arange(CTX, dtype=np.int64)[:, None]
        k = (j * TC + np.arange(TC, dtype=np.int64))[None, :]
        ang = (2.0 * np.pi / CTX) * ((n * k) % CTX).astype(np.float64)
        sc = 1.0 / math.sqrt(CTX)
        C = np.stack([np.cos(ang) * sc, np.sin(ang) * sc], axis=1).astype(np.float32).astype(NPBF)
        _CACHE[key] = (np.ascontiguousarray(L), np.ascontiguousarray(C))
    return _CACHE[key]


def host_consts2():
    k = np.arange(128)
    c2 = np.stack([np.ones((128, 128), np.float32), np.full((128, 128), 1.0 / 128, np.float32)], axis=1)
    swap = (k[:, None] == ((k[None, :] + 64) % 128)).astype(np.float32)
    return _bf(c2), np.ascontiguousarray(swap)


def prep_B(inp, l, resA):
    consts, _ = host_consts()
    c2, swap = host_consts2()
    lam_init = 0.8 - 0.6 * math.exp(-0.3 * l)
    lconst = np.tile(np.array([[lam_init, 1.0 - lam_init]], np.float32), (128, 1))
    maps = []
    gathered = {}
    for b in range(2):
        rs = [resA[4 * b + j] for j in range(4)]

        def gcols(name):
            return np.ascontiguousarray(np.concatenate([r[name][:, TL:T] for r in rs] + [r[name][:, :TL] for r in rs], axis=1))

        def grows(name):
            return np.ascontiguousarray(np.concatenate([r[name][TL:T] for r in rs] + [r[name][:TL] for r in rs], axis=0))

        gathered[b] = dict(ka_all=gcols("ka"), kb_all=gcols("kb"), va_all=grows("va"), vb_all=grows("vb"),
                           w12_all=grows("w12"))
        lat = gathered[b]["w12_all"][CTX:]
        ridx = (SEQ - np.arange(SEQ // 2)) % SEQ
        gathered[b]["w12_rev"] = np.ascontiguousarray(lat[ridx])
    for r in range(NCORE):
        b, j = r // 4, r % 4
        L, C = host_dft(j)
        o = resA[r]
        m = dict(hmid=o["hmid"], hn=o["hn"], mv=o["mv"], qa=o["qa"], qb=o["qb"], dftL=L, dftC=C,
                 w_in=inp["w_in"][l], w_branch=inp["w_branch"][l], w_out=inp["w_out"][l],
                 wi=inp["ffn_wi"][l, 1], wo=inp["ffn_wo"][l, 1],
                 dlam=np.ascontiguousarray(inp["diff_lam"][l].reshape(256)),
                 subg=np.ascontiguousarray(inp["diff_subln_g"][l].reshape(128, 1)),
                 lconst=lconst, consts=consts, consts2=c2, swap=swap)
        m.update(gathered[b])
        maps.append(m)
    return maps


_PROGS = {}


def _prog(name):
    if name not in _PROGS:
        _PROGS[name] = (build_A if name == "A" else build_B)()[0]
    return _PROGS[name]


def kernel(x, c, ctx, c_ctx, w_ada, b_ada, norm_g, ffn_wi, ffn_wo, w_in, qk_g, diff_lam, diff_subln_g,
           w_branch, w_out):
    inp = dict(x=x, c=c, ctx=ctx, c_ctx=c_ctx, w_ada=w_ada, b_ada=b_ada, norm_g=norm_g, ffn_wi=ffn_wi,
               ffn_wo=ffn_wo, w_in=w_in, qk_g=qk_g, diff_lam=diff_lam, diff_subln_g=diff_subln_g,
               w_branch=w_branch, w_out=w_out)
    inp = {k: np.ascontiguousarray(np.asarray(v, dtype=np.float32)) for k, v in inp.items()}
    cores = list(range(NCORE))
    hin = first_hin(inp)
    for l in range(2):
        resA = run_bass_kernel_spmd(_prog("A"), prep_A(inp, l, hin), core_ids=cores).results
        resB = run_bass_kernel_spmd(_prog("B"), prep_B(inp, l, resA), core_ids=cores).results
        hin = [r["hout"] for r in resB]
    out = np.empty((2, SEQ, D), np.float32)
    for r in range(NCORE):
        b, j = r // 4, r % 4
        out[b, j * TL:(j + 1) * TL, :] = hin[r][:, :TL].T
    return out
```

```python
import contextlib
import math
import numpy as np
import ml_dtypes
import concourse.bass as bass
import concourse.mybir as mybir
from concourse.bass_utils import run_bass_kernel_spmd

F32 = mybir.dt.float32
BF16 = mybir.dt.bfloat16
AF = mybir.ActivationFunctionType
ALU = mybir.AluOpType
NPBF = ml_dtypes.bfloat16

D = 1024
SEQ = 8192
CTX = 256
DFF = 2816
INW = 5888
NCORE = 8
TL = 2048
TC = 64
T = TL + TC
NKEY = SEQ + CTX
EPS = 1e-6
BLKS = [(0, 512), (512, 512), (1024, 512), (1536, 512), (2048, 64)]
TGS = [[0, 1], [2, 3, 4]]

ENGS = ("pe", "act", "dve", "pool", "sp")
SEM_CAP = 30000
DMA_POOL = 20
DEBUG = False


class Sched:
    def __init__(self, nc):
        self.nc = nc
        self.ops = []
        self.lastw = {}
        self.readers = {}
        self.last_of = {e: None for e in ENGS}
        self.open_dma = set()

    def op(self, eng, fn, reads=(), writes=(), dma=False, extra=()):
        i = len(self.ops)
        deps = set(extra)
        for t in reads:
            w = self.lastw.get(t)
            if w is not None:
                deps.add(w)
        for t in writes:
            w = self.lastw.get(t)
            if w is not None:
                deps.add(w)
            rd = self.readers.get(t)
            if rd:
                deps.update(rd.values())
        self.ops.append(dict(eng=eng, fn=fn, dma=dma, deps=deps, inc=False))
        for t in reads:
            key = ("dma", i) if dma else eng
            self.readers.setdefault(t, {})[key] = i
        for t in writes:
            self.lastw[t] = i
            self.readers[t] = {}
        if fn is not None:
            self.last_of[eng] = i
        if dma:
            self.open_dma.add(i)
        return i

    def barrier(self):
        deps = set(v for v in self.last_of.values() if v is not None) | set(self.open_dma)
        self.open_dma = set()
        for e in ENGS:
            self.op(e, None, extra=deps)

    def emit(self):
        nc = self.nc
        ops = self.ops
        for o in ops:
            best = {}
            keep = set()
            for d in o["deps"]:
                p = ops[d]
                if p["fn"] is None:
                    continue
                if p["eng"] == "pe" and o["eng"] == "pe" and not p["dma"]:
                    continue
                if p["dma"]:
                    keep.add(d)
                else:
                    e = p["eng"]
                    if e not in best or d > best[e]:
                        best[e] = d
            keep.update(best.values())
            o["deps"] = keep
        waited_eng = {e: {p: -1 for p in ENGS} for e in ENGS}
        waited_dma = {e: set() for e in ENGS}
        for i, o in enumerate(ops):
            e = o["eng"]
            real = []
            for d in sorted(o["deps"]):
                p = ops[d]
                if p["dma"]:
                    if d in waited_dma[e]:
                        continue
                    waited_dma[e].add(d)
                    real.append(d)
                else:
                    if waited_eng[e][p["eng"]] >= d:
                        continue
                    waited_eng[e][p["eng"]] = d
                    real.append(d)
            o["waits"] = real
            for d in real:
                ops[d]["inc"] = True
        n_inc = {e: 0 for e in ENGS}
        for o in ops:
            if o["inc"] and not o["dma"]:
                n_inc[o["eng"]] += 1
        stack = contextlib.ExitStack()
        sems = {}
        for e in ENGS:
            k = (n_inc[e] + SEM_CAP - 1) // SEM_CAP
            sems[e] = [stack.enter_context(nc.semaphore(f"s_{e}_{j}")) for j in range(max(k, 1))]
        dma_sems = {e: [stack.enter_context(nc.semaphore(f"d_{e}_{j}")) for j in range(DMA_POOL)]
                    for e in ("sp", "pool", "act")}
        cnt = {e: 0 for e in ENGS}
        dma_cnt = {e: 0 for e in ENGS}
        dma_uses = {e: [0] * DMA_POOL for e in ENGS}
        per_eng = {e: [] for e in ENGS}
        for i, o in enumerate(ops):
            e = o["eng"]
            o["pre_wait"] = None
            if o["dma"]:
                j = dma_cnt[e] % DMA_POOL
                dma_cnt[e] += 1
                prev = dma_uses[e][j]
                if prev > 0:
                    o["pre_wait"] = (dma_sems[e][j], 16 * prev)
                dma_uses[e][j] = prev + 1
                o["sem"] = (dma_sems[e][j], 16 * (prev + 1))
            elif o["inc"]:
                c = cnt[e]
                cnt[e] += 1
                o["sem"] = (sems[e][c // SEM_CAP], c % SEM_CAP + 1)
            per_eng[e].append(i)

        def run_engine(e, eng):
            for i in per_eng[e]:
                o = ops[i]
                if o["pre_wait"] is not None:
                    eng.wait_ge(*o["pre_wait"])
                for d in o["waits"]:
                    s, v = ops[d]["sem"]
                    eng.wait_ge(s, v)
                if o["fn"] is None:
                    continue
                ins = o["fn"](eng)
                if o["dma"]:
                    ins.then_inc(o["sem"][0], 16)
                elif o["inc"]:
                    ins.then_inc(o["sem"][0], 1)

        with nc.Block() as block:
            @block.tensor
            def _(eng):
                run_engine("pe", eng)

            @block.scalar
            def _(eng):
                run_engine("act", eng)

            @block.vector
            def _(eng):
                run_engine("dve", eng)

            @block.gpsimd
            def _(eng):
                run_engine("pool", eng)

            @block.sync
            def _(eng):
                run_engine("sp", eng)
        stack.close()
        return {e: len(per_eng[e]) for e in ENGS}


class Arena:
    def __init__(self, nc, st, words, parent=None, base=0):
        if parent is None:
            self.t = st.enter_context(nc.sbuf_tensor("arena", [128, words], F32))
        else:
            self.t = parent.t
        self.words = base + words
        self.off = base

    def sub(self, base, words):
        return Arena(None, None, words, parent=self, base=base)

    def mark(self):
        return self.off

    def release(self, m):
        self.off = m

    def alloc(self, shape, dt):
        n = int(np.prod(shape))
        w = n if dt == F32 else (n + 1) // 2
        w = (w + 15) // 16 * 16
        assert self.off + w <= self.words, f"arena overflow {self.off}+{w}>{self.words}"
        v = self.t[:, self.off:self.off + w]
        self.off += w
        if dt != F32:
            v = v.bitcast(dt)
        v = v[:, 0:n]
        if len(shape) == 2:
            return v.rearrange("p (a b) -> p a b", a=shape[0])
        if len(shape) == 3:
            return v.rearrange("p (a b c) -> p a b c", a=shape[0], b=shape[1])
        return v


class Ctx:
    pass


def emit_norm(S, K, h, hn, blk_ids, k, col_base):
    for bi in blk_ids:
        c0, n = BLKS[bi]
        m = 0 if bi < 4 else 1
        sq = K.sq[bi % 2]
        for c in range(8):
            S.op("act", lambda e, c=c, sq=sq, c0=c0, n=n: e.activation(
                out=sq[:, c, 0:n], in_=h[:, c, c0:c0 + n], func=AF.Square),
                reads=[("h", c, bi)], writes=[("sq", bi % 2, c)])
        ps = K.ps[K.nps % 8]
        pst = ("ps", K.nps % 8)
        K.nps += 1
        for c in range(8):
            S.op("pe", lambda e, c=c, sq=sq, ps=ps, n=n: e.matmul(
                ps[:, 0:n], lhsT=K.ones_full, rhs=sq[:, c, 0:n], start=(c == 0), stop=(c == 7)),
                reads=[("sq", bi % 2, c), ("const",)], writes=[pst])
        rstd = K.rstd[bi % 2]
        S.op("act", lambda e, ps=ps, rstd=rstd, n=n: e.activation(
            out=rstd[:, 0:n], in_=ps[:, 0:n], func=AF.Sqrt, bias=K.epsc, scale=1.0),
            reads=[pst, ("const2",)], writes=[("rstd", bi % 2)])
        S.op("dve", lambda e, rstd=rstd, n=n: e.reciprocal(out=rstd[:, 0:n], in_=rstd[:, 0:n]),
            reads=[("rstd", bi % 2)], writes=[("rstd", bi % 2)])
        for c in range(8):
            tmp = K.tmp[c % 2]
            S.op("dve", lambda e, c=c, tmp=tmp, rstd=rstd, c0=c0, n=n, m=m: e.scalar_tensor_tensor(
                out=tmp[:, 0:n], in0=h[:, c, c0:c0 + n], scalar=K.mv[:, 0 * 3 + k, c, m:m + 1],
                in1=rstd[:, 0:n], op0=ALU.mult, op1=ALU.mult),
                reads=[("h", c, bi), ("rstd", bi % 2), ("mv",)], writes=[("tmp", c % 2)])
            S.op("act", lambda e, c=c, tmp=tmp, c0=c0, n=n, m=m: e.activation(
                out=hn[:, c, c0 - col_base:c0 - col_base + n], in_=tmp[:, 0:n], func=AF.Identity,
                bias=K.mv[:, 1 * 3 + k, c, m:m + 1], scale=1.0),
                reads=[("tmp", c % 2), ("mv",)], writes=[("hn", c, bi)])


def emit_ffn(S, K, h, k, wi_d, wo_d, A):
    mk = A.mark()
    for tg in TGS:
        _ffn_group(S, K, h, k, wi_d, wo_d, A, tg)
    A.release(mk)


def _ffn_group(S, K, h, k, wi_d, wo_d, A, tg):
    if True:
        col_base = BLKS[tg[0]][0]
        ncol = sum(BLKS[b][1] for b in tg)
        m2 = A.mark()
        hn = A.alloc([8, ncol], BF16)
        act = A.alloc([22, ncol], BF16)
        wib = [A.alloc([8, 256], BF16) for _ in range(2)]
        wob = [A.alloc([22, 128], BF16) for _ in range(2)]
        sg = [A.alloc([512], F32) for _ in range(2)]
        emit_norm(S, K, h, hn, tg, k, col_base)
        wiv = wi_d.rearrange("(kc p) n -> p kc n", p=128)
        for j in range(22):
            wb = wib[j % 2]
            S.op("pool", lambda e, wb=wb, j=j: e.dma_start(out=wb[:, :, 0:128], in_=wiv[:, :, j * 128:(j + 1) * 128]),
                 writes=[("wib", j % 2, 0)], dma=True)
            S.op("pool", lambda e, wb=wb, j=j: e.dma_start(out=wb[:, :, 128:256],
                                                          in_=wiv[:, :, DFF + j * 128:DFF + (j + 1) * 128]),
                 writes=[("wib", j % 2, 1)], dma=True)
            for bi in tg:
                c0, n = BLKS[bi]
                l0 = c0 - col_base
                pg = K.ps[K.nps % 8]
                pgt = ("ps", K.nps % 8)
                K.nps += 1
                pu = K.ps[K.nps % 8]
                put = ("ps", K.nps % 8)
                K.nps += 1
                for half, (pp, ppt) in enumerate(((pg, pgt), (pu, put))):
                    for kc in range(8):
                        S.op("pe", lambda e, pp=pp, wb=wb, kc=kc, half=half, l0=l0, n=n: e.matmul(
                            pp[:, 0:n], lhsT=wb[:, kc, half * 128:(half + 1) * 128], rhs=hn[:, kc, l0:l0 + n],
                            start=(kc == 0), stop=(kc == 7)),
                            reads=[("wib", j % 2, half), ("hn", kc, bi)], writes=[ppt])
                sgt = sg[K.nsg % 2]
                sgtok = ("sg", K.nsg % 2)
                K.nsg += 1
                S.op("act", lambda e, pg=pg, sgt=sgt, n=n: e.activation(out=sgt[:, 0:n], in_=pg[:, 0:n], func=AF.Silu),
                     reads=[pgt], writes=[sgtok])
                S.op("dve", lambda e, pu=pu, sgt=sgt, j=j, l0=l0, n=n: e.tensor_tensor(
                    out=act[:, j, l0:l0 + n], in0=pu[:, 0:n], in1=sgt[:, 0:n], op=ALU.mult),
                    reads=[put, sgtok], writes=[("act", j, bi)])
        if getattr(K, "dbg", None) is not None and tg[0] == 0:
            dd = K.dbg
            S.op("sp", lambda e: e.dma_start(out=dd["act"], in_=act), reads=[("act", j, b) for j in range(22) for b in tg], writes=[("dbg", 0)], dma=True)
            S.op("sp", None, reads=[("dbg", 0)])
            raise StopIteration
        wov = wo_d.rearrange("(kc p) n -> p kc n", p=128)
        for oc in range(8):
            wb = wob[oc % 2]
            S.op("pool", lambda e, wb=wb, oc=oc: e.dma_start(out=wb[:, 0:11, :], in_=wov[:, 0:11, oc * 128:(oc + 1) * 128]),
                 writes=[("wob", oc % 2, 0)], dma=True)
            S.op("pool", lambda e, wb=wb, oc=oc: e.dma_start(out=wb[:, 11:22, :], in_=wov[:, 11:22, oc * 128:(oc + 1) * 128]),
                 writes=[("wob", oc % 2, 1)], dma=True)
            for bi in tg:
                c0, n = BLKS[bi]
                l0 = c0 - col_base
                m = 0 if bi < 4 else 1
                po = K.ps[K.nps % 8]
                pot = ("ps", K.nps % 8)
                K.nps += 1
                for kc in range(22):
                    S.op("pe", lambda e, po=po, wb=wb, kc=kc, l0=l0, n=n: e.matmul(
                        po[:, 0:n], lhsT=wb[:, kc, :], rhs=act[:, kc, l0:l0 + n], start=(kc == 0), stop=(kc == 21)),
                        reads=[("wob", oc % 2, kc // 11), ("act", kc, bi)], writes=[pot])
                S.op("dve", lambda e, po=po, oc=oc, c0=c0, n=n, m=m: e.scalar_tensor_tensor(
                    out=h[:, oc, c0:c0 + n], in0=po[:, 0:n], scalar=K.mv[:, 2 * 3 + k, oc, m:m + 1],
                    in1=h[:, oc, c0:c0 + n], op0=ALU.mult, op1=ALU.add),
                    reads=[pot, ("h", oc, bi), ("mv",)], writes=[("h", oc, bi)])
        S.barrier()
        A.release(m2)


def setup_common(nc, st, S):
    K = Ctx()
    K.pbig = st.enter_context(nc.psum_tensor("pbig", [128, 4096], F32))
    K.ps = [K.pbig[:, i * 512:(i + 1) * 512] for i in range(8)]
    K.nps = 0
    K.nsg = 0
    return K


def load_consts(nc, S, K, A, consts_d):
    cst = A.alloc([3, 128], BF16)
    S.op("sp", lambda e: e.dma_start(out=cst, in_=consts_d), writes=[("const",)], dma=True)
    K.ones_full = cst[:, 0, :]
    K.ones_blk = cst[:, 1, :]
    K.perm = cst[:, 2, :]
    K.epsc = A.alloc([1], F32)
    S.op("dve", lambda e: e.memset(K.epsc, EPS), writes=[("const2",)])
    K.sq = [A.alloc([8, 512], BF16) for _ in range(2)]
    K.rstd = [A.alloc([512], F32) for _ in range(2)]
    K.tmp = [A.alloc([512], F32) for _ in range(2)]


def build_A(chain=None):
    pfx = "n_" if chain else ""
    nc = chain["nc"] if chain else bass.Bass("TRN2", target_bir_lowering=False)

    def din(name, shape, dt=F32):
        return nc.dram_tensor(pfx + name, shape, dt, kind="ExternalInput").ap()

    def dout(name, shape, dt=F32):
        return nc.dram_tensor(pfx + name, shape, dt, kind="ExternalOutput").ap()

    hin = None if chain else din("hin", [D, T])
    cs_d = din("cs", [128, 8, 2])
    wada = din("w_ada", [D, 9 * D])
    bada = din("b_ada", [128, 72])
    ng_d = din("norm_g", [128, 3, 8])
    wi_d = din("wi", [D, 2 * DFF])
    wo_d = din("wo", [DFF, D])
    win_d = din("w_in", [D, INW])
    qkg_d = din("qkg", [128, 4])
    cos_d = din("cos", [128, T])
    sin_d = din("sin", [128, T])
    consts_d = None if chain else din("consts", [128, 3, 128], BF16)
    cdft_d = din("cdft", [128, 256], BF16)

    hmid = dout("hmid", [D, T])
    hn_o = dout("hn", [D, T], BF16)
    mv_o = dout("mv", [128, 9, 8, 2])
    qa_o = dout("qa", [512, T], BF16)
    qb_o = dout("qb", [512, T], BF16)
    ka_o = dout("ka", [128, T], BF16)
    kb_o = dout("kb", [512, T], BF16)
    va_o = dout("va", [T, 128], BF16)
    vb_o = dout("vb", [T, 512], BF16)
    w12_o = dout("w12", [T, 4, 256], BF16)
    out_tokens = []

    S = chain["S"] if chain else Sched(nc)
    with contextlib.ExitStack() as st:
        if chain:
            K, A, h = chain["K"], chain["A"], chain["h"]
        else:
            K = setup_common(nc, st, S)
            A = Arena(nc, st, 52800)
            h = A.alloc([8, T], F32)
            K.mv = A.alloc([9, 8, 2], F32)
            load_consts(nc, S, K, A, consts_d)
            hv = hin.rearrange("(c p) t -> p c t", p=128)
            for c in range(8):
                S.op("sp", lambda e, c=c: e.dma_start(out=h[:, c, :], in_=hv[:, c, :]),
                     writes=[("h", c, b) for b in range(5)], dma=True)

        mk = A.mark()
        cs_t = A.alloc([8, 2], F32)
        sc = A.alloc([8, 2], F32)
        bt = A.alloc([72], F32)
        gt = A.alloc([3, 8], F32)
        mod = A.alloc([72, 2], F32)
        stg = [A.alloc([8, 512], F32) for _ in range(4)]
        S.op("sp", lambda e: e.dma_start(out=cs_t, in_=cs_d), writes=[("cs",)], dma=True)
        S.op("sp", lambda e: e.dma_start(out=bt, in_=bada), writes=[("bt",)], dma=True)
        S.op("sp", lambda e: e.dma_start(out=gt, in_=ng_d), writes=[("gt",)], dma=True)
        S.op("act", lambda e: e.activation(out=sc, in_=cs_t, func=AF.Silu), reads=[("cs",)], writes=[("sc",)])
        wav = wada.rearrange("(kc p) n -> p kc n", p=128)
        pm = K.ps[0]
        for cg in range(18):
            sb_ = stg[cg % 4]
            S.op("sp", lambda e, sb_=sb_, cg=cg: e.dma_start(out=sb_, in_=wav[:, :, cg * 512:(cg + 1) * 512]),
                 writes=[("stg", cg % 4)], dma=True)
            for j in range(4):
                ch = cg * 4 + j
                for kc in range(8):
                    S.op("pe", lambda e, sb_=sb_, j=j, kc=kc, ch=ch: e.matmul(
                        pm[:, 2 * ch:2 * ch + 2], lhsT=sb_[:, kc, j * 128:(j + 1) * 128], rhs=sc[:, kc, :],
                        start=(kc == 0), stop=(kc == 7)),
                        reads=[("stg", cg % 4), ("sc",)], writes=[("ps", 0)])
        pmv = pm[:, 0:144].rearrange("p (j m) -> p j m", m=2)
        for m in range(2):
            S.op("dve", lambda e, m=m: e.tensor_tensor(out=mod[:, :, m], in0=pmv[:, :, m], in1=bt, op=ALU.add),
                 reads=[("ps", 0), ("bt",)], writes=[("mod", m)])
        K.nps = 1
        for k in range(3):
            for m in range(2):
                S.op("dve", lambda e, k=k, m=m: e.scalar_tensor_tensor(
                    out=K.mv[:, 0 * 3 + k, :, m], in0=mod[:, (3 * k + 1) * 8:(3 * k + 2) * 8, m], scalar=1.0,
                    in1=gt[:, k, :], op0=ALU.add, op1=ALU.mult),
                    reads=[("mod", m), ("gt",)], writes=[("mv",)])
                S.op("dve", lambda e, k=k, m=m: e.tensor_copy(
                    out=K.mv[:, 1 * 3 + k, :, m], in_=mod[:, (3 * k) * 8:(3 * k + 1) * 8, m]),
                    reads=[("mod", m)], writes=[("mv",)])
                S.op("dve", lambda e, k=k, m=m: e.tensor_scalar(
                    out=K.mv[:, 2 * 3 + k, :, m], in0=mod[:, (3 * k + 2) * 8:(3 * k + 3) * 8, m],
                    scalar1=(1.0 if k == 1 else 0.5), scalar2=None, op0=ALU.mult),
                    reads=[("mod", m)], writes=[("mv",)])
        S.op("sp", lambda e: e.dma_start(out=mv_o, in_=K.mv), reads=[("mv",)], writes=[("o_mv",)], dma=True)
        out_tokens.append(("o_mv",))
        S.barrier()
        A.release(mk)

        if DEBUG:
            K.dbg = dict(act=dout("dbg_act", [128, 22, 1024], BF16))
            try:
                emit_ffn(S, K, h, 0, wi_d, wo_d, A)
            except StopIteration:
                pass
            stats = S.emit()
            return nc, stats
        emit_ffn(S, K, h, 0, wi_d, wo_d, A)

        hn = A.alloc([8, T], BF16)
        emit_norm(S, K, h, hn, range(5), 1, 0)
        hmv = hmid.rearrange("(c p) t -> p c t", p=128)
        hnv = hn_o.rearrange("(c p) t -> p c t", p=128)
        for c in range(8):
            S.op("sp", lambda e, c=c: e.dma_start(out=hmv[:, c, :], in_=h[:, c, :]),
                 reads=[("h", c, b) for b in range(5)], writes=[("o_h", c)], dma=True)
            S.op("sp", lambda e, c=c: e.dma_start(out=hnv[:, c, :], in_=hn[:, c, :]),
                 reads=[("hn", c, b) for b in range(5)], writes=[("o_hn", c)], dma=True)
            out_tokens += [("o_h", c), ("o_hn", c)]
        S.barrier()
        HA = A.sub(0, 8 * T)
        cos_t = HA.alloc([T], F32)
        sin_t = HA.alloc([T], F32)
        qkg = A.alloc([4], F32)
        qkg2 = A.alloc([4], F32)
        cdft = A.alloc([256], BF16)
        S.op("sp", lambda e: e.dma_start(out=cos_t, in_=cos_d), writes=[("cos",)], dma=True)
        S.op("sp", lambda e: e.dma_start(out=sin_t, in_=sin_d), writes=[("sin",)], dma=True)
        S.op("sp", lambda e: e.dma_start(out=qkg, in_=qkg_d), writes=[("qkg",)], dma=True)
        S.op("sp", lambda e: e.dma_start(out=cdft, in_=cdft_d), writes=[("cdft",)], dma=True)
        S.op("dve", lambda e: e.tensor_copy(out=qkg2, in_=qkg), reads=[("qkg",)], writes=[("qkg2",)])
        for gi in (0, 2):
            S.op("dve", lambda e, gi=gi: e.tensor_scalar(out=qkg2[:, gi:gi + 1], in0=qkg[:, gi:gi + 1], scalar1=0.125,
                                                       scalar2=None, op0=ALU.mult),
                 reads=[("qkg",), ("qkg2",)], writes=[("qkg2",)])
        winv = win_d.rearrange("(kc p) n -> p kc n", p=128)
        wqb = [A.alloc([8, 128], BF16) for _ in range(3)]
        NS = 4
        qsq = [A.alloc([512], BF16) for _ in range(NS)]
        pgb = [A.alloc([512], BF16) for _ in range(NS)]
        qrs = [A.alloc([512], F32) for _ in range(NS)]
        t1 = [A.alloc([512], F32) for _ in range(NS)]
        t2 = [A.alloc([512], F32) for _ in range(NS)]
        ostg = [A.alloc([T], BF16) for _ in range(2)]
        chunks = []
        for i in range(4):
            chunks.append((i * 128, 0, qa_o[i * 128:(i + 1) * 128, :]))
        chunks.append((512, 1, ka_o[:, :]))
        for i in range(4):
            chunks.append((768 + i * 128, 2, qb_o[i * 128:(i + 1) * 128, :]))
        for i in range(4):
            chunks.append((1280 + i * 128, 3, kb_o[i * 128:(i + 1) * 128, :]))
        items = []
        for ci, (col0, gi, oap) in enumerate(chunks):
            for bi in range(5):
                items.append((ci, col0, gi, oap, bi))
        PPB, PMB, PRB = (0, 1), (2, 3), (4, 5, 6)

        def qk_stage(it, sidx, stg_):
            ci, col0, gi, oap, bi = items[it]
            c0, n = BLKS[bi]
            b4 = it % NS
            wb = wqb[ci % 3]
            og = ostg[ci % 2]
            ppi, pmi, pri = PPB[it % 2], PMB[it % 2], PRB[it % 3]
            pp, pmm, pr = K.ps[ppi], K.ps[pmi], K.ps[pri]
            if stg_ == 0:
                if bi == 0:
                    S.op("pool", lambda e: e.dma_start(out=wb, in_=winv[:, :, col0:col0 + 128]),
                         writes=[("wqb", ci % 3)], dma=True)
                for kc in range(8):
                    S.op("pe", lambda e, kc=kc: e.matmul(
                        pp[:, 0:n], lhsT=wb[:, kc, :], rhs=hn[:, kc, c0:c0 + n], start=(kc == 0), stop=(kc == 7)),
                        reads=[("wqb", ci % 3), ("hn", kc, bi)], writes=[("ps", ppi)])
            elif stg_ == 1:
                S.op("act", lambda e: e.activation(out=qsq[b4][:, 0:n], in_=pp[:, 0:n], func=AF.Square),
                     reads=[("ps", ppi)], writes=[("qsq", b4)])
                S.op("act", lambda e: e.activation(out=pgb[b4][:, 0:n], in_=pp[:, 0:n], func=AF.Identity,
                                                   scale=qkg2[:, gi:gi + 1]),
                     reads=[("ps", ppi), ("qkg2",)], writes=[("pgb", b4)])
            elif stg_ == 2:
                S.op("pe", lambda e: e.matmul(pmm[:, 0:n], lhsT=K.ones_blk, rhs=qsq[b4][:, 0:n], start=True, stop=True),
                     reads=[("qsq", b4), ("const",)], writes=[("ps", pmi)])
                S.op("pe", lambda e: e.matmul(pr[:, 0:n], lhsT=K.perm, rhs=pgb[b4][:, 0:n], start=True, stop=True),
                     reads=[("pgb", b4), ("const",)], writes=[("ps", pri)])
            elif stg_ == 3:
                S.op("act", lambda e: e.activation(out=qrs[b4][:, 0:n], in_=pmm[:, 0:n], func=AF.Ln, bias=K.epsc, scale=1.0),
                     reads=[("ps", pmi), ("const2",)], writes=[("qrs", b4)])
                S.op("act", lambda e: e.activation(out=qrs[b4][:, 0:n], in_=qrs[b4][:, 0:n], func=AF.Exp, scale=-0.5),
                     reads=[("qrs", b4)], writes=[("qrs", b4)])
                S.op("pool", lambda e: e.tensor_tensor(out=t1[b4][:, 0:n], in0=pgb[b4][:, 0:n], in1=cos_t[:, c0:c0 + n], op=ALU.mult),
                     reads=[("pgb", b4), ("cos",)], writes=[("t1", b4)])
            elif stg_ == 4:
                S.op("dve", lambda e: e.tensor_tensor(out=t2[b4][:, 0:n], in0=pr[:, 0:n], in1=sin_t[:, c0:c0 + n], op=ALU.mult),
                     reads=[("ps", pri), ("sin",)], writes=[("t2", b4)])
                S.op("pool", lambda e: e.tensor_tensor(out=t1[b4][:, 0:n], in0=t1[b4][:, 0:n], in1=t2[b4][:, 0:n], op=ALU.add),
                     reads=[("t1", b4), ("t2", b4)], writes=[("t1", b4)])
            else:
                S.op("dve", lambda e: e.tensor_tensor(out=og[:, c0:c0 + n], in0=t1[b4][:, 0:n], in1=qrs[b4][:, 0:n], op=ALU.mult),
                     reads=[("t1", b4), ("qrs", b4)], writes=[("ostg", ci % 2, bi)])
                if bi == 4:
                    S.op("sp", lambda e: e.dma_start(out=oap, in_=og),
                         reads=[("ostg", ci % 2, b) for b in range(5)], writes=[("o_q", ci)], dma=True)
                    out_tokens.append(("o_q", ci))

        NST = 6
        for t in range(len(items) + NST - 1):
            for sg2 in range(NST):
                it = t - sg2
                if 0 <= it < len(items):
                    qk_stage(it, it, sg2)
        K.nps = 7

        w12stg = HA.alloc([17, 4, 256], BF16)
        for g in range(4):
            ci = 13 + g
            wb = wqb[ci % 3]
            col0 = 2304 + g * 128
            S.op("pool", lambda e, wb=wb, col0=col0: e.dma_start(out=wb, in_=winv[:, :, col0:col0 + 128]),
                 writes=[("wqb", ci % 3)], dma=True)
            ug = ostg[ci % 2]
            for bi in range(5):
                c0, n = BLKS[bi]
                pp = K.ps[K.nps % 8]; ppt = ("ps", K.nps % 8); K.nps += 1
                for kc in range(8):
                    S.op("pe", lambda e, pp=pp, wb=wb, kc=kc, c0=c0, n=n: e.matmul(
                        pp[:, 0:n], lhsT=wb[:, kc, :], rhs=hn[:, kc, c0:c0 + n], start=(kc == 0), stop=(kc == 7)),
                        reads=[("wqb", ci % 3), ("hn", kc, bi)], writes=[ppt])
                S.op("act", lambda e, pp=pp, ug=ug, c0=c0, n=n: e.activation(out=ug[:, c0:c0 + n], in_=pp[:, 0:n],
                                                                            func=AF.Identity),
                     reads=[ppt], writes=[("ostg", ci % 2, bi)])
            for tt in range(17):
                nt = 128 if tt < 16 else 64
                pp = K.ps[K.nps % 8]; ppt = ("ps", K.nps % 8); K.nps += 1
                S.op("pe", lambda e, pp=pp, ug=ug, tt=tt, nt=nt: e.matmul(
                    pp[0:nt, 0:256], lhsT=ug[:, tt * 128:tt * 128 + nt], rhs=cdft, start=True, stop=True),
                    reads=[("ostg", ci % 2, tt // 4), ("cdft",)], writes=[ppt])
                S.op("dve", lambda e, pp=pp, tt=tt, nt=nt, g=g: e.tensor_copy(out=w12stg[0:nt, tt, g, :], in_=pp[0:nt, 0:256]),
                     reads=[ppt], writes=[("w12stg", tt)])
        w12v = w12_o[0:2048].rearrange("(tt p) g c -> p tt g c", p=128)
        S.op("sp", lambda e: e.dma_start(out=w12v, in_=w12stg[:, 0:16]), reads=[("w12stg", tt) for tt in range(16)],
             writes=[("o_w12", 0)], dma=True)
        S.op("sp", lambda e: e.dma_start(out=w12_o[2048:2112], in_=w12stg[0:64, 16]), reads=[("w12stg", 16)],
             writes=[("o_w12", 1)], dma=True)
        out_tokens += [("o_w12", 0), ("o_w12", 1)]

        wva = A.alloc([8, 128], BF16)
        wvb = A.alloc([8, 512], BF16)
        vstg = A.alloc([17, 640], BF16)
        S.op("pool", lambda e: e.dma_start(out=wva, in_=winv[:, :, 640:768]), writes=[("wva",)], dma=True)
        for q4 in range(4):
            S.op("pool", lambda e, q4=q4: e.dma_start(out=wvb[:, 2 * q4:2 * q4 + 2, :], in_=winv[:, 2 * q4:2 * q4 + 2, 1792:2304]),
                 writes=[("wvb", q4)], dma=True)
        for tt in range(17):
            nt = 128 if tt < 16 else 64
            pa = K.ps[K.nps % 8]; pat = ("ps", K.nps % 8); K.nps += 1
            pb = K.ps[K.nps % 8]; pbt = ("ps", K.nps % 8); K.nps += 1
            for kc in range(8):
                S.op("pe", lambda e, pa=pa, kc=kc, tt=tt, nt=nt: e.matmul(
                    pa[0:nt, 0:128], lhsT=hn[:, kc, tt * 128:tt * 128 + nt], rhs=wva[:, kc, :], start=(kc == 0), stop=(kc == 7)),
                    reads=[("wva",), ("hn", kc, tt // 4)], writes=[pat])
            for kc in range(8):
                S.op("pe", lambda e, pb=pb, kc=kc, tt=tt, nt=nt: e.matmul(
                    pb[0:nt, 0:512], lhsT=hn[:, kc, tt * 128:tt * 128 + nt], rhs=wvb[:, kc, :], start=(kc == 0), stop=(kc == 7)),
                    reads=[("wvb", kc // 2), ("hn", kc, tt // 4)], writes=[pbt])
            S.op("dve", lambda e, pa=pa, tt=tt, nt=nt: e.tensor_copy(out=vstg[0:nt, tt, 0:128], in_=pa[0:nt, 0:128]),
                 reads=[pat], writes=[("vstg", tt, 0)])
            S.op("act", lambda e, pb=pb, tt=tt, nt=nt: e.activation(out=vstg[0:nt, tt, 128:640], in_=pb[0:nt, 0:512],
                                                                  func=AF.Identity),
                 reads=[pbt], writes=[("vstg", tt, 1)])
        vav = va_o[0:2048].rearrange("(tt p) c -> p tt c", p=128)
        vbv = vb_o[0:2048].rearrange("(tt p) c -> p tt c", p=128)
        S.op("sp", lambda e: e.dma_start(out=vav, in_=vstg[:, 0:16, 0:128]), reads=[("vstg", tt, 0) for tt in range(16)],
             writes=[("o_v", 0)], dma=True)
        S.op("sp", lambda e: e.dma_start(out=vbv, in_=vstg[:, 0:16, 128:640]), reads=[("vstg", tt, 1) for tt in range(16)],
             writes=[("o_v", 1)], dma=True)
        S.op("sp", lambda e: e.dma_start(out=va_o[2048:2112], in_=vstg[0:64, 16, 0:128]), reads=[("vstg", 16, 0)],
             writes=[("o_v", 2)], dma=True)
        S.op("sp", lambda e: e.dma_start(out=vb_o[2048:2112], in_=vstg[0:64, 16, 128:640]), reads=[("vstg", 16, 1)],
             writes=[("o_v", 3)], dma=True)
        out_tokens += [("o_v", i) for i in range(4)]
        S.op("sp", None, reads=out_tokens)
        if chain:
            return None
        stats = S.emit()
    return nc, stats


def _bf(a):
    return np.asarray(a, dtype=np.float32).astype(NPBF)


def host_consts():
    k = np.arange(128)
    ones_full = np.full((128, 128), 1.0 / 1024, np.float32)
    ones_blk = ((k[:, None] // 64) == (k[None, :] // 64)).astype(np.float32) / 64.0
    perm = np.zeros((128, 128), np.float32)
    for m in range(128):
        if m % 64 < 32:
            perm[m + 32, m] = -1.0
        else:
            perm[m - 32, m] = 1.0
    consts = np.stack([ones_full, ones_blk, perm], axis=1)
    ang = 2.0 * np.pi * ((k[:, None] * k[None, :]) % 128) / 128.0
    cdft = np.concatenate([np.cos(ang), -np.sin(ang)], axis=1) / np.sqrt(128.0)
    return _bf(consts), _bf(cdft)


def host_rope(core):
    j = core % 4
    n = np.arange(j * TL, (j + 1) * TL)
    row = (n // 64).astype(np.float32)
    col = (n % 64).astype(np.float32)
    inv = np.power(np.float32(10000.0), -np.arange(16, dtype=np.float32) / np.float32(16)).astype(np.float32)
    ang = np.concatenate([row[:, None] * inv, col[:, None] * inv], axis=-1).astype(np.float32)
    cos = np.ones((128, T), np.float32)
    sin = np.zeros((128, T), np.float32)
    p = np.arange(128) % 32
    cos[:, :TL] = np.cos(ang).astype(np.float32).T[p]
    sin[:, :TL] = np.sin(ang).astype(np.float32).T[p]
    return cos, sin


def prep_A(inp, l, hin_cores):
    consts, cdft = host_consts()
    maps = []
    for r in range(NCORE):
        b = r // 4
        cs = np.stack([inp["c"][b].reshape(8, 128).T, inp["c_ctx"].reshape(8, 128).T], axis=-1)
        cos, sin = host_rope(r)
        maps.append({
            "hin": np.ascontiguousarray(hin_cores[r], dtype=np.float32),
            "cs": np.ascontiguousarray(cs, dtype=np.float32),
            "w_ada": inp["w_ada"][l],
            "b_ada": np.ascontiguousarray(inp["b_ada"][l].reshape(72, 128).T),
            "norm_g": np.ascontiguousarray(inp["norm_g"][l].reshape(3, 8, 128).transpose(2, 0, 1)),
            "wi": inp["ffn_wi"][l, 0],
            "wo": inp["ffn_wo"][l, 0],
            "w_in": inp["w_in"][l],
            "qkg": np.ascontiguousarray(np.tile(inp["qk_g"][l].T, (2, 1))),
            "cos": cos, "sin": sin, "consts": consts, "cdft": cdft,
        })
    return maps


def first_hin(inp):
    out = []
    for r in range(NCORE):
        b, j = r // 4, r % 4
        hx = inp["x"][b, j * TL:(j + 1) * TL, :].T
        hc = inp["ctx"][b, j * TC:(j + 1) * TC, :].T
        out.append(np.ascontiguousarray(np.concatenate([hx, hc], axis=1)))
    return out


def build_B(chain_A=False):
    nc = bass.Bass("TRN2", target_bir_lowering=False)

    def din(name, shape, dt=F32):
        return nc.dram_tensor(name, shape, dt, kind="ExternalInput").ap()

    def dout(name, shape, dt=F32):
        return nc.dram_tensor(name, shape, dt, kind="ExternalOutput").ap()

    hmid = din("hmid", [D, T])
    hn_d = din("hn", [D, T], BF16)
    mv_d = din("mv", [128, 9, 8, 2])
    qa_d = din("qa", [512, T], BF16)
    qb_d = din("qb", [512, T], BF16)
    ka_d = din("ka_all", [128, NKEY], BF16)
    kb_d = din("kb_all", [512, NKEY], BF16)
    va_d = din("va_all", [NKEY, 128], BF16)
    vb_d = din("vb_all", [NKEY, 512], BF16)
    w12_d = din("w12_all", [NKEY, 4, 256], BF16)
    dftL = din("dftL", [SEQ // 2 + 128, 2, TL], BF16)
    w12r_d = din("w12_rev", [SEQ // 2, 4, 256], BF16)
    dftC = din("dftC", [CTX, 2, TC], BF16)
    win_d = din("w_in", [D, INW])
    wbr_d = din("w_branch", [3, 512, D])
    wout_d = din("w_out", [D, D])
    wi_d = din("wi", [D, 2 * DFF])
    wo_d = din("wo", [DFF, D])
    dlam_d = din("dlam", [256])
    subg_d = din("subg", [128, 1])
    lconst_d = din("lconst", [128, 2])
    consts_d = din("consts", [128, 3, 128], BF16)
    consts2_d = din("consts2", [128, 2, 128], BF16)
    swap_d = din("swap", [128, 128])
    hout = None if chain_A else dout("hout", [D, T])

    S = Sched(nc)
    with contextlib.ExitStack() as st:
        K = setup_common(nc, st, S)
        A = Arena(nc, st, 52800)
        h = A.alloc([8, T], F32)
        HA = A.sub(0, 8 * T)
        K.mv = A.alloc([9, 8, 2], F32)
        load_consts(nc, S, K, A, consts_d)
        S.op("sp", lambda e: e.dma_start(out=K.mv, in_=mv_d), writes=[("mv",)], dma=True)
        mk0 = A.mark()
        c2 = A.alloc([2, 128], BF16)
        S.op("sp", lambda e: e.dma_start(out=c2, in_=consts2_d), writes=[("c2",)], dma=True)
        ones1 = c2[:, 0, :]
        ones128 = c2[:, 1, :]
        swp = A.alloc([128], F32)
        S.op("sp", lambda e: e.dma_start(out=swp, in_=swap_d), writes=[("swp",)], dma=True)
        br = A.alloc([12, T], BF16)

        dl = A.alloc([256], F32)
        lc = A.alloc([2], F32)
        sg_ = A.alloc([1], F32)
        lw = A.alloc([8], F32)
        S.op("sp", lambda e: e.dma_start(out=dl, in_=dlam_d.partition_broadcast(128)), writes=[("dl",)], dma=True)
        S.op("sp", lambda e: e.dma_start(out=lc, in_=lconst_d), writes=[("lc",)], dma=True)
        S.op("sp", lambda e: e.dma_start(out=sg_, in_=subg_d), writes=[("subg",)], dma=True)
        prod = A.alloc([128], F32)
        S.op("dve", lambda e: e.tensor_tensor(out=prod[:, 0:64], in0=dl[:, 0:64], in1=dl[:, 64:128], op=ALU.mult),
             reads=[("dl",)], writes=[("prod", 0)])
        S.op("dve", lambda e: e.tensor_tensor(out=prod[:, 64:128], in0=dl[:, 128:192], in1=dl[:, 192:256], op=ALU.mult),
             reads=[("dl",)], writes=[("prod", 1)])
        for i in range(2):
            S.op("dve", lambda e, i=i: e.reduce_sum(out=lw[:, 4 + i:5 + i], in_=prod[:, i * 64:(i + 1) * 64],
                                                   axis=mybir.AxisListType.X),
                 reads=[("prod", i)], writes=[("lw", 4 + i)])
            S.op("act", lambda e, i=i: e.activation(out=lw[:, i:i + 1], in_=lw[:, 4 + i:5 + i], func=AF.Exp),
                 reads=[("lw", 4 + i)], writes=[("lw", i)])
        S.op("dve", lambda e: e.tensor_tensor(out=lw[:, 6:7], in0=lw[:, 1:2], in1=lw[:, 0:1], op=ALU.subtract),
             reads=[("lw", 0), ("lw", 1)], writes=[("lw", 6)])
        S.op("dve", lambda e: e.tensor_tensor(out=lw[:, 2:3], in0=lw[:, 6:7], in1=lc[:, 0:1], op=ALU.subtract),
             reads=[("lw", 6), ("lc",)], writes=[("lw", 2)])
        S.op("dve", lambda e: e.tensor_tensor(out=lw[:, 3:4], in0=sg_, in1=lc[:, 1:2], op=ALU.mult),
             reads=[("subg",), ("lc",)], writes=[("lw", 3)])

        mk = A.mark()
        qt = [A.alloc([T], BF16) for _ in range(2)]
        vaug = A.alloc([66, 192], BF16)
        Eb = [A.alloc([2, 512], BF16) for _ in range(3)]
        Osb = [A.alloc([512], F32) for _ in range(4)]
        Osb2 = [A.alloc([512], F32) for _ in range(2)]
        sqb = A.alloc([512], BF16)
        Es = [[A.alloc([512], F32) for _ in range(2)] for _ in range(2)]
        onesf = A.alloc([128], F32)
        S.op("pool", lambda e: e.memset(onesf, 1.0), writes=[("onesf",)])
        kbuf = [HA.alloc([NKEY], BF16) for _ in range(2)]
        vbuf = [HA.alloc([66, 128], BF16) for _ in range(2)]
        SB = (0, 1, 2)
        AB = (3, 4, 5, 6)
        MB = 7
        cnt = dict(s=0, e=0, a=0, o=0, q=0, k=0, v=0)
        S.op("pool", lambda e: e.memset(vaug[:, :, 0:64], 1.0), writes=[("vaug", "o1")])
        S.op("pool", lambda e: e.memset(vaug[:, :, 128:192], 1.0), writes=[("vaug", "o2")])

        def attend(qtile, q_tok, p0, kt_src, k_tok, pv_list, qblocks):
            pass

        LAG = 2
        steps = []
        pend = []

        def run_steps():
            nstep = len(steps)
            for i in range(nstep + LAG):
                if i < nstep:
                    stp = steps[i]
                    sp_ = i % 2
                    ei = i % 3
                    n = stp["n"]
                    for hf in range(2):
                        bk = 2 * sp_ + hf
                        S.op("pe", (lambda e, stp=stp, bk=bk, hf=hf: stp["qk"][hf](e, K.ps[bk])),
                             reads=stp["qk_reads"][hf], writes=[("ps", bk)])
                    spair = K.pbig[:, sp_ * 1024:(sp_ + 1) * 1024].rearrange("p (b c) -> p b c", b=2)
                    S.op("act", lambda e, spair=spair, ei=ei, n=n: e.activation(out=Eb[ei][:, :, 0:n], in_=spair[:, :, 0:n], func=AF.Exp),
                         reads=[("ps", 2 * sp_), ("ps", 2 * sp_ + 1)], writes=[("E", ei)])
                j = i - LAG
                if j >= 0:
                    stp = steps[j]
                    ei = j % 3
                    for (fn, hf, rd, wr) in stp["pv"]:
                        S.op("pe", (lambda e, fn=fn, ei=ei, hf=hf: fn(e, Eb[ei][:, hf, :])), reads=rd + [("E", ei)], writes=wr)
                    for (fn, hf, rd, wr) in stp.get("dve", ()):
                        S.op("dve", (lambda e, fn=fn, ei=ei, hf=hf: fn(e, Eb[ei][:, hf, :])), reads=rd + [("E", ei)], writes=wr)
                    if stp["post"] is not None:
                        stp["post"]()
                for pd in pend:
                    pd[0] -= 1
                for pd in [p_ for p_ in pend if p_[0] <= 0]:
                    pd[1]()
                    pend.remove(pd)
            for pd in list(pend):
                pd[1]()
            pend.clear()
            steps.clear()

        kav = ka_d
        vav = va_d.rearrange("(kt p) c -> p kt c", p=128)
        for c in range(4):
            run_steps()
            hk = c // 2
            qi = cnt["q"] % 2
            cnt["q"] += 1
            qtl = qt[qi]
            S.op("sp", lambda e, qtl=qtl, c=c: e.dma_start(out=qtl, in_=qa_d[c * 128:(c + 1) * 128, :]),
                 writes=[("qt", qi)], dma=True)
            if c % 2 == 0:
                ki = cnt["k"] % 2
                cnt["k"] += 1
                kb_ = kbuf[ki]
                for half in range(2):
                    S.op("sp", lambda e, kb_=kb_, half=half, hk=hk: e.dma_start(
                        out=kb_[half * 64:(half + 1) * 64, :], in_=kav[hk * 64:(hk + 1) * 64, :]),
                        writes=[("kbuf", ki, half)], dma=True)
                S.op("sp", lambda e, hk=hk: e.dma_start(out=vaug[:, :, 64:128], in_=vav[:, :, hk * 64:(hk + 1) * 64]),
                     writes=[("vaug", "v")], dma=True)
            for bi in range(5):
                c0, n = BLKS[bi]
                nkt = 66 if bi < 4 else 2

                def post_a(c=c, c0=c0, n=n, bi=bi):
                    for hh in range(2):
                        p0 = hh * 64
                        s0 = 64 - p0
                        ab = 4 + hh
                        acc = K.ps[ab]
                        oi = cnt["o"] % 4
                        cnt["o"] += 1
                        ob = Osb[oi]
                        mb = 6 + oi % 2
                        S.op("act", lambda e, ob=ob, acc=acc: e.activation(out=ob[:, 0:n], in_=acc[:, 0:n], func=AF.Identity),
                             reads=[("ps", ab)], writes=[("osb", oi)])
                        S.op("dve", lambda e, ob=ob, s0=s0: e.reciprocal(out=ob[s0:s0 + 64, 0:n], in_=ob[s0:s0 + 64, 0:n]),
                             reads=[("osb", oi)], writes=[("osb", oi)])

                        def fin(ob=ob, oi=oi, mb=mb, p0=p0, hh=hh):
                            S.op("pe", lambda e: e.matmul(K.ps[mb][:, 0:n], lhsT=swp, rhs=ob[:, 0:n], start=True, stop=True),
                                 reads=[("osb", oi), ("swp",)], writes=[("ps", mb)])
                            S.op("dve", lambda e: e.tensor_tensor(
                                out=br[p0:p0 + 64, c, c0:c0 + n], in0=K.ps[mb][p0:p0 + 64, 0:n], in1=ob[p0:p0 + 64, 0:n], op=ALU.mult),
                                reads=[("ps", mb), ("osb", oi)], writes=[("br", c, bi, hh)])
                        pend.append([3, fin])

                for kt in range(nkt):
                    qk = []
                    qkr = []
                    pv = []
                    for hh in range(2):
                        p0 = hh * 64
                        vsl = (64, 192) if hh == 0 else (0, 128)
                        qk.append(lambda e, sps, kb_=kb_, qtl=qtl, kt=kt, p0=p0, c0=c0, n=n: e.matmul(
                            sps[:, 0:n], lhsT=kb_[p0:p0 + 64, kt * 128:(kt + 1) * 128], rhs=qtl[p0:p0 + 64, c0:c0 + n],
                            start=True, stop=True))
                        qkr.append([("kbuf", ki, hh), ("qt", qi)])
                        pv.append(((lambda e, Et, hh=hh, kt=kt, vsl=vsl, n=n, nkt=nkt: e.matmul(
                            K.ps[4 + hh][:, 0:n], lhsT=vaug[:, kt, vsl[0]:vsl[1]], rhs=Et[:, 0:n],
                            start=(kt == 0), stop=(kt == nkt - 1))),
                            hh, [("vaug", "v"), ("vaug", "o1"), ("vaug", "o2")], [("ps", 4 + hh)]))
                    steps.append(dict(n=n, qk=qk, qk_reads=qkr, pv=pv, post=(post_a if kt == nkt - 1 else None)))
        run_steps()

        vbv = vb_d.rearrange("(kt p) c -> p kt c", p=128)
        for hb in range(4):
            run_steps()
            qi = cnt["q"] % 2
            cnt["q"] += 1
            qtl = qt[qi]
            S.op("sp", lambda e, qtl=qtl, hb=hb: e.dma_start(out=qtl, in_=qb_d[hb * 128:(hb + 1) * 128, :]),
                 writes=[("qt", qi)], dma=True)
            ki = cnt["k"] % 2
            cnt["k"] += 1
            kb_ = kbuf[ki]
            S.op("sp", lambda e, kb_=kb_, hb=hb: e.dma_start(out=kb_, in_=kb_d[hb * 128:(hb + 1) * 128, :]),
                 writes=[("kbuf", ki, 0), ("kbuf", ki, 1)], dma=True)
            vi = cnt["v"] % 2
            cnt["v"] += 1
            vb_ = vbuf[vi]
            S.op("sp", lambda e, vb_=vb_, hb=hb: e.dma_start(out=vb_, in_=vbv[:, :, hb * 128:(hb + 1) * 128]),
                 writes=[("vbuf", vi)], dma=True)
            for bi in range(5):
                c0, n = BLKS[bi]
                nkt = 66 if bi < 4 else 2

                def post_b(hb=hb, c0=c0, n=n, bi=bi):
                    S.op("dve", lambda e: e.tensor_copy(out=Osb2[0][:, 0:n], in_=K.ps[4][:, 0:n]),
                         reads=[("ps", 4)], writes=[("osb2", 0)])
                    S.op("dve", lambda e: e.tensor_copy(out=Osb2[1][:, 0:n], in_=K.ps[6][:, 0:n]),
                         reads=[("ps", 6)], writes=[("osb2", 1)])
                    S.op("dve", lambda e: e.tensor_copy(out=Osb[1][:, 0:n], in_=K.ps[7][:, 0:n]),
                         reads=[("ps", 7)], writes=[("osb", 1)])
                    S.op("pe", lambda e: e.matmul(K.ps[5][:, 0:n], lhsT=onesf, rhs=Es[bi % 2][0][:, 0:n], start=True, stop=True),
                         reads=[("es", bi % 2, 0), ("onesf",)], writes=[("ps", 5)])
                    S.op("dve", lambda e: e.reciprocal(out=Osb[0][:, 0:n], in_=K.ps[5][:, 0:n]),
                         reads=[("ps", 5)], writes=[("osb", 0)])
                    S.op("dve", lambda e: e.reciprocal(out=Osb[1][:, 0:n], in_=Osb[1][:, 0:n]),
                         reads=[("osb", 1)], writes=[("osb", 1)])
                    for mp in range(2):
                        S.op("dve", lambda e, mp=mp: e.tensor_tensor(out=Osb2[mp][:, 0:n], in0=Osb2[mp][:, 0:n],
                                                                    in1=Osb[mp][:, 0:n], op=ALU.mult),
                             reads=[("osb2", mp), ("osb", mp)], writes=[("osb2", mp)])
                    S.op("dve", lambda e: e.scalar_tensor_tensor(out=Osb[0][:, 0:n], in0=Osb2[1][:, 0:n], scalar=lw[:, 2:3],
                                                                in1=Osb2[0][:, 0:n], op0=ALU.mult, op1=ALU.add),
                         reads=[("osb2", 0), ("osb2", 1), ("lw", 2), ("osb", 0)], writes=[("osb", 0)])
                    S.op("act", lambda e: e.activation(out=sqb[:, 0:n], in_=Osb[0][:, 0:n], func=AF.Square),
                         reads=[("osb", 0)], writes=[("sqb",)])
                    S.op("pe", lambda e: e.matmul(K.ps[5][:, 0:n], lhsT=ones128, rhs=sqb[:, 0:n], start=True, stop=True),
                         reads=[("sqb",), ("c2",)], writes=[("ps", 5)])
                    S.op("act", lambda e: e.activation(out=Osb[1][:, 0:n], in_=K.ps[5][:, 0:n], func=AF.Sqrt, bias=K.epsc, scale=1.0),
                         reads=[("ps", 5), ("const2",)], writes=[("osb", 1)])
                    S.op("dve", lambda e: e.reciprocal(out=Osb[1][:, 0:n], in_=Osb[1][:, 0:n]),
                         reads=[("osb", 1)], writes=[("osb", 1)])
                    S.op("dve", lambda e: e.scalar_tensor_tensor(
                        out=br[:, 4 + hb, c0:c0 + n], in0=Osb[0][:, 0:n], scalar=lw[:, 3:4], in1=Osb[1][:, 0:n],
                        op0=ALU.mult, op1=ALU.mult),
                        reads=[("osb", 0), ("osb", 1), ("lw", 3)], writes=[("br", 4 + hb, bi, 0), ("br", 4 + hb, bi, 1)])

                for kt in range(nkt):
                    qk = []
                    qkr = []
                    pv = []
                    sm = []
                    for mp in range(2):
                        p0 = mp * 64
                        qk.append(lambda e, sps, kb_=kb_, qtl=qtl, kt=kt, p0=p0, c0=c0, n=n: e.matmul(
                            sps[:, 0:n], lhsT=kb_[p0:p0 + 64, kt * 128:(kt + 1) * 128], rhs=qtl[p0:p0 + 64, c0:c0 + n],
                            start=True, stop=True))
                        qkr.append([("kbuf", ki, mp), ("qt", qi)])
                        pv.append(((lambda e, Et, kt=kt, vb_=vb_, mp=mp, n=n, nkt=nkt: e.matmul(
                            K.ps[4 + 2 * mp][:, 0:n], lhsT=vb_[:, kt, :], rhs=Et[:, 0:n],
                            start=(kt == 0), stop=(kt == nkt - 1))), mp, [("vbuf", vi)], [("ps", 4 + 2 * mp)]))
                        est = Es[bi % 2][mp]
                        if mp == 1:
                            pv.append(((lambda e, Et, kt=kt, mp=mp, n=n, nkt=nkt: e.matmul(
                                K.ps[5 + 2 * mp][:, 0:n], lhsT=ones1, rhs=Et[:, 0:n],
                                start=(kt == 0), stop=(kt == nkt - 1))), mp, [("c2",)], [("ps", 5 + 2 * mp)]))
                        elif kt == 0:
                            sm.append(((lambda e, Et, est=est, n=n: e.tensor_copy(out=est[:, 0:n], in_=Et[:, 0:n])),
                                       mp, [], [("es", bi % 2, mp)]))
                        else:
                            sm.append(((lambda e, Et, est=est, n=n: e.tensor_tensor(out=est[:, 0:n], in0=Et[:, 0:n], in1=est[:, 0:n], op=ALU.add)),
                                       mp, [("es", bi % 2, mp)], [("es", bi % 2, mp)]))
                    steps.append(dict(n=n, qk=qk, qk_reads=qkr, pv=pv, dve=sm, post=(post_b if kt == nkt - 1 else None)))
        run_steps()
        S.barrier()
        A.release(mk)

        mk = A.mark()
        w12h = HA.sub(0, 8 * T).alloc([32, 4, 256], BF16)
        dtile = [A.alloc([2, 512], BF16) for _ in range(4)]
        pa = [A.alloc([4, 4, 256], BF16) for _ in range(2)]
        pb = [A.alloc([4, 4, 256], BF16) for _ in range(2)]
        x32 = A.alloc([4, 256], BF16)
        w12v = w12_d[CTX:NKEY].rearrange("(t p) g c -> p t g c", p=128)
        w12rv = w12r_d.rearrange("(t p) g c -> p t g c", p=128)
        for pc in range(8):
            a_, b_ = pa[pc % 2], pb[pc % 2]
            S.op("sp", lambda e, a_=a_, pc=pc: e.dma_start(out=a_, in_=w12v[:, pc * 4:(pc + 1) * 4]),
                 writes=[("pa", pc % 2)], dma=True)
            S.op("sp", lambda e, b_=b_, pc=pc: e.dma_start(out=b_, in_=w12rv[:, pc * 4:(pc + 1) * 4]),
                 writes=[("pb", pc % 2)], dma=True)
            S.op("dve", lambda e, a_=a_, b_=b_, pc=pc: e.tensor_tensor(
                out=w12h[:, pc * 4:(pc + 1) * 4, :, 0:128], in0=a_[:, :, :, 0:128], in1=b_[:, :, :, 0:128], op=ALU.add),
                reads=[("pa", pc % 2), ("pb", pc % 2)], writes=[("w12h", pc, 0)])
            S.op("pool", lambda e, a_=a_, b_=b_, pc=pc: e.tensor_tensor(
                out=w12h[:, pc * 4:(pc + 1) * 4, :, 128:256], in0=a_[:, :, :, 128:256], in1=b_[:, :, :, 128:256], op=ALU.subtract),
                reads=[("pa", pc % 2), ("pb", pc % 2)], writes=[("w12h", pc, 1)])
        S.op("sp", lambda e: e.dma_start(out=x32, in_=w12v[:, 32]), writes=[("x32",)], dma=True)
        nd = 0
        for qb_ in range(4):
            c0, n = BLKS[qb_]
            for t in range(33):
                di = nd % 4
                nd += 1
                S.op("sp", lambda e, di=di, t=t, c0=c0: e.dma_start(
                    out=dtile[di], in_=dftL[t * 128:(t + 1) * 128, :, c0:c0 + 512]),
                    writes=[("dtile", di)], dma=True)
                for g in range(4):
                    for cs_ in range(2 if t < 32 else 1):
                        lh = (w12h[:, t, g, cs_ * 128:(cs_ + 1) * 128] if t < 32 else x32[:, g, 0:128])
                        rd = ([("w12h", t // 4, cs_)] if t < 32 else [("x32",)])
                        S.op("pe", lambda e, g=g, cs_=cs_, di=di, lh=lh, t=t: e.matmul(
                            K.ps[AB[g]][:, 0:512], lhsT=lh, rhs=dtile[di][:, cs_, :],
                            start=(t == 0 and cs_ == 0), stop=(t == 32)),
                            reads=rd + [("dtile", di)], writes=[("ps", AB[g])])
            for g in range(4):
                S.op("act" if g % 2 else "dve",
                     (lambda e, g=g, c0=c0: e.activation(out=br[:, 8 + g, c0:c0 + 512], in_=K.ps[AB[g]][:, 0:512], func=AF.Identity))
                     if g % 2 else
                     (lambda e, g=g, c0=c0: e.tensor_copy(out=br[:, 8 + g, c0:c0 + 512], in_=K.ps[AB[g]][:, 0:512])),
                     reads=[("ps", AB[g])], writes=[("br", 8 + g, qb_, 0), ("br", 8 + g, qb_, 1)])
        w12c = w12_d[0:CTX].rearrange("(t p) g c -> p t g c", p=128)
        dct = A.alloc([2, 2, TC], BF16)
        w12cx = A.alloc([2, 4, 256], BF16)
        S.op("sp", lambda e: e.dma_start(out=w12cx, in_=w12c), writes=[("w12cx",)], dma=True)
        S.op("sp", lambda e: e.dma_start(out=dct, in_=dftC.rearrange("(t p) s k -> p t s k", p=128)),
             writes=[("dct",)], dma=True)
        for g in range(4):
            for t in range(2):
                for cs_ in range(2):
                    S.op("pe", lambda e, g=g, cs_=cs_, t=t: e.matmul(
                        K.ps[AB[g]][:, 0:TC], lhsT=w12cx[:, t, g, cs_ * 128:(cs_ + 1) * 128], rhs=dct[:, t, cs_, :],
                        start=(t == 0 and cs_ == 0), stop=(t == 1 and cs_ == 1)),
                        reads=[("w12cx",), ("dct",)], writes=[("ps", AB[g])])
            S.op("dve", lambda e, g=g: e.tensor_copy(out=br[:, 8 + g, TL:T], in_=K.ps[AB[g]][:, 0:TC]),
                 reads=[("ps", AB[g])], writes=[("br", 8 + g, 4, 0), ("br", 8 + g, 4, 1)])
        S.barrier()
        A.release(mk)

        hv = hmid.rearrange("(c p) t -> p c t", p=128)
        for c in range(8):
            S.op("sp", lambda e, c=c: e.dma_start(out=h[:, c, :], in_=hv[:, c, :]),
                 writes=[("h", c, b) for b in range(5)], dma=True)
        K.nps = 0
        hnv = hn_d.rearrange("(c p) t -> p c t", p=128)
        winv = win_d.rearrange("(kc p) n -> p kc n", p=128)
        wbv = wbr_d.rearrange("i (ec p) d -> p i ec d", p=128)
        woutv = wout_d.rearrange("(kc p) n -> p kc n", p=128)
        for tg in TGS:
            _merge_group(S, K, A, h, br, tg, hnv, winv, wbv, woutv)

        A.release(mk0)
        emit_ffn(S, K, h, 2, wi_d, wo_d, A)
        if chain_A:
            build_A(chain=dict(nc=nc, S=S, K=K, A=A, h=h))
        else:
            hov = hout.rearrange("(c p) t -> p c t", p=128)
            outs = []
            for c in range(8):
                S.op("sp", lambda e, c=c: e.dma_start(out=hov[:, c, :], in_=h[:, c, :]),
                     reads=[("h", c, b) for b in range(5)], writes=[("o_h", c)], dma=True)
                outs.append(("o_h", c))
            S.op("sp", None, reads=outs)
        stats = S.emit()
    return nc, stats


def _merge_group(S, K, A, h, br, tg, hnv, winv, wbv, woutv):
    col_base = BLKS[tg[0]][0]
    ncol = sum(BLKS[b][1] for b in tg)
    m2 = A.mark()
    hng = A.alloc([8, ncol], BF16)
    mrg = A.alloc([8, ncol], BF16)
    acc = A.alloc([ncol], F32)
    gw = [A.alloc([8, 128], BF16) for _ in range(2)]
    bw = [A.alloc([4, 128], BF16) for _ in range(2)]
    ow = [A.alloc([8, 128], BF16) for _ in range(2)]
    sig = [A.alloc([512], F32) for _ in range(2)]
    tmpm = [A.alloc([512], F32) for _ in range(2)]
    for c in range(8):
        S.op("sp", lambda e, c=c: e.dma_start(out=hng[:, c, :], in_=hnv[:, c, col_base:col_base + ncol]),
             writes=[("hng", c)], dma=True)
    nw = 0
    ns = 0
    for dc in range(8):
        for i in range(3):
            wi_ = nw % 2
            nw += 1
            gcol = 2816 + i * 1024 + dc * 128
            S.op("pool", lambda e, wi_=wi_, gcol=gcol: e.dma_start(out=gw[wi_], in_=winv[:, :, gcol:gcol + 128]),
                 writes=[("gw", wi_)], dma=True)
            S.op("pool", lambda e, wi_=wi_, i=i, dc=dc: e.dma_start(out=bw[wi_], in_=wbv[:, i, :, dc * 128:(dc + 1) * 128]),
                 writes=[("bw", wi_)], dma=True)
            for bi in tg:
                c0, n = BLKS[bi]
                l0 = c0 - col_base
                pg = K.ps[K.nps % 8]; pgt = ("ps", K.nps % 8); K.nps += 1
                pp = K.ps[K.nps % 8]; ppt = ("ps", K.nps % 8); K.nps += 1
                si = ns % 2
                ns += 1
                for kc in range(8):
                    S.op("pe", lambda e, pg=pg, wi_=wi_, kc=kc, l0=l0, n=n: e.matmul(
                        pg[:, 0:n], lhsT=gw[wi_][:, kc, :], rhs=hng[:, kc, l0:l0 + n], start=(kc == 0), stop=(kc == 7)),
                        reads=[("gw", wi_), ("hng", kc)], writes=[pgt])
                for ec in range(4):
                    S.op("pe", lambda e, pp=pp, wi_=wi_, ec=ec, i=i, c0=c0, n=n: e.matmul(
                        pp[:, 0:n], lhsT=bw[wi_][:, ec, :], rhs=br[:, 4 * i + ec, c0:c0 + n], start=(ec == 0), stop=(ec == 3)),
                        reads=[("bw", wi_), ("br", 4 * i + ec, bi, 0), ("br", 4 * i + ec, bi, 1)], writes=[ppt])
                S.op("act", lambda e, pg=pg, si=si, n=n: e.activation(out=sig[si][:, 0:n], in_=pg[:, 0:n], func=AF.Sigmoid),
                     reads=[pgt], writes=[("sig", si)])
                if i == 0:
                    S.op("dve", lambda e, pp=pp, si=si, l0=l0, n=n: e.tensor_tensor(
                        out=acc[:, l0:l0 + n], in0=pp[:, 0:n], in1=sig[si][:, 0:n], op=ALU.mult),
                        reads=[ppt, ("sig", si)], writes=[("acc", bi)])
                else:
                    S.op("dve", lambda e, pp=pp, si=si, n=n: e.tensor_tensor(
                        out=tmpm[si][:, 0:n], in0=pp[:, 0:n], in1=sig[si][:, 0:n], op=ALU.mult),
                        reads=[ppt, ("sig", si)], writes=[("tmpm", si)])
                    if i == 1:
                        S.op("dve", lambda e, si=si, l0=l0, n=n: e.tensor_tensor(
                            out=acc[:, l0:l0 + n], in0=acc[:, l0:l0 + n], in1=tmpm[si][:, 0:n], op=ALU.add),
                            reads=[("acc", bi), ("tmpm", si)], writes=[("acc", bi)])
                    else:
                        S.op("dve", lambda e, si=si, l0=l0, n=n, dc=dc: e.tensor_tensor(
                            out=mrg[:, dc, l0:l0 + n], in0=acc[:, l0:l0 + n], in1=tmpm[si][:, 0:n], op=ALU.add),
                            reads=[("acc", bi), ("tmpm", si)], writes=[("mrg", dc, bi)])
    for oc in range(8):
        wi_ = oc % 2
        S.op("pool", lambda e, wi_=wi_, oc=oc: e.dma_start(out=ow[wi_], in_=woutv[:, :, oc * 128:(oc + 1) * 128]),
             writes=[("ow", wi_)], dma=True)
        for bi in tg:
            c0, n = BLKS[bi]
            l0 = c0 - col_base
            m = 0 if bi < 4 else 1
            po = K.ps[K.nps % 8]; pot = ("ps", K.nps % 8); K.nps += 1
            for kc in range(8):
                S.op("pe", lambda e, po=po, wi_=wi_, kc=kc, l0=l0, n=n: e.matmul(
                    po[:, 0:n], lhsT=ow[wi_][:, kc, :], rhs=mrg[:, kc, l0:l0 + n], start=(kc == 0), stop=(kc == 7)),
                    reads=[("ow", wi_), ("mrg", kc, bi)], writes=[pot])
            S.op("dve", lambda e, po=po, oc=oc, c0=c0, n=n, m=m: e.scalar_tensor_tensor(
                out=h[:, oc, c0:c0 + n], in0=po[:, 0:n], scalar=K.mv[:, 2 * 3 + 1, oc, m:m + 1],
                in1=h[:, oc, c0:c0 + n], op0=ALU.mult, op1=ALU.add),
                reads=[pot, ("h", oc, bi), ("mv",)], writes=[("h", oc, bi)])
    S.barrier()
    A.release(m2)


_CACHE = {}


def host_dft(j):
    key = ("dft", j)
    if key not in _CACHE:
        NH = SEQ // 2
        n = np.arange(NH + 128, dtype=np.int64)[:, None]
        k = (j * TL + np.arange(TL, dtype=np.int64))[None, :]
        ang = (2.0 * np.pi / SEQ) * ((n * k) % SEQ).astype(np.float64)
        sc = 1.0 / math.sqrt(SEQ)
        cs_tab = np.cos(ang) * sc
        sn_tab = np.sin(ang) * sc
        cs_tab[0] *= 0.5
        cs_tab[NH + 1:] = 0.0
        sn_tab[NH:] = 0.0
        L = np.stack([cs_tab, sn_tab], axis=1).astype(np.float32).astype(NPBF)
        n = np.arange(CTX, dtype=np.int64)[:, None]
        k = (j * TC + np.arange(TC, dtype=np.int64))[None, :]
        ang = (2.0 * np.pi / CTX) * ((n * k) % CTX).astype(np.float64)
        sc = 1.0 / math.sqrt(CTX)
        C = np.stack([np.cos(ang) * sc, np.sin(ang) * sc], axis=1).astype(np.float32).astype(NPBF)
        _CACHE[key] = (np.ascontiguousarray(L), np.ascontiguousarray(C))
    return _CACHE[key]


def host_consts2():
    k = np.arange(128)
    c2 = np.stack([np.ones((128, 128), np.float32), np.full((128, 128), 1.0 / 128, np.float32)], axis=1)
    swap = (k[:, None] == ((k[None, :] + 64) % 128)).astype(np.float32)
    return _bf(c2), np.ascontiguousarray(swap)


def prep_B(inp, l, resA):
    consts, _ = host_consts()
    c2, swap = host_consts2()
    lam_init = 0.8 - 0.6 * math.exp(-0.3 * l)
    lconst = np.tile(np.array([[lam_init, 1.0 - lam_init]], np.float32), (128, 1))
    maps = []
    gathered = {}
    for b in range(2):
        rs = [resA[4 * b + j] for j in range(4)]

        def gcols(name):
            return np.ascontiguousarray(np.concatenate([r[name][:, TL:T] for r in rs] + [r[name][:, :TL] for r in rs], axis=1))

        def grows(name):
            return np.ascontiguousarray(np.concatenate([r[name][TL:T] for r in rs] + [r[name][:TL] for r in rs], axis=0))

        gathered[b] = dict(ka_all=gcols("ka"), kb_all=gcols("kb"), va_all=grows("va"), vb_all=grows("vb"),
                           w12_all=grows("w12"))
        lat = gathered[b]["w12_all"][CTX:]
        ridx = (SEQ - np.arange(SEQ // 2)) % SEQ
        gathered[b]["w12_rev"] = np.ascontiguousarray(lat[ridx])
    for r in range(NCORE):
        b, j = r // 4, r % 4
        L, C = host_dft(j)
        o = resA[r]
        m = dict(hmid=o["hmid"], hn=o["hn"], mv=o["mv"], qa=o["qa"], qb=o["qb"], dftL=L, dftC=C,
                 w_in=inp["w_in"][l], w_branch=inp["w_branch"][l], w_out=inp["w_out"][l],
                 wi=inp["ffn_wi"][l, 1], wo=inp["ffn_wo"][l, 1],
                 dlam=np.ascontiguousarray(inp["diff_lam"][l].reshape(256)),
                 subg=np.ascontiguousarray(inp["diff_subln_g"][l].reshape(128, 1)),
                 lconst=lconst, consts=consts, consts2=c2, swap=swap)
        m.update(gathered[b])
        maps.append(m)
    return maps


_PROGS = {}


def _prog(name):
    if name not in _PROGS:
        _PROGS[name] = {"A": build_A, "B": build_B, "BA": (lambda: build_B(chain_A=True))}[name]()[0]
    return _PROGS[name]


def kernel(x, c, ctx, c_ctx, w_ada, b_ada, norm_g, ffn_wi, ffn_wo, w_in, qk_g, diff_lam, diff_subln_g,
           w_branch, w_out):
    inp = dict(x=x, c=c, ctx=ctx, c_ctx=c_ctx, w_ada=w_ada, b_ada=b_ada, norm_g=norm_g, ffn_wi=ffn_wi,
               ffn_wo=ffn_wo, w_in=w_in, qk_g=qk_g, diff_lam=diff_lam, diff_subln_g=diff_subln_g,
               w_branch=w_branch, w_out=w_out)
    inp = {k: np.ascontiguousarray(np.asarray(v, dtype=np.float32)) for k, v in inp.items()}
    cores = list(range(NCORE))
    hin = first_hin(inp)
    resA0 = run_bass_kernel_spmd(_prog("A"), prep_A(inp, 0, hin), core_ids=cores).results
    mapsBA = prep_B(inp, 0, resA0)
    a1 = prep_A(inp, 1, hin)
    for r in range(NCORE):
        mapsBA[r].update({"n_" + k: v for k, v in a1[r].items() if k not in ("hin", "consts")})
    resBA = run_bass_kernel_spmd(_prog("BA"), mapsBA, core_ids=cores).results
    resA1 = [{k[2:]: v for k, v in r.items() if k.startswith("n_")} for r in resBA]
    resB = run_bass_kernel_spmd(_prog("B"), prep_B(inp, 1, resA1), core_ids=cores).results
    hin = [r["hout"] for r in resB]
    out = np.empty((2, SEQ, D), np.float32)
    for r in range(NCORE):
        b, j = r // 4, r % 4
        out[b, j * TL:(j + 1) * TL, :] = hin[r][:, :TL].T
    return out
```
